# Optimizing a Trainium2 kernel written in Bass

```python
import math
import jax
import jax.numpy as jnp
from jax import lax
import numpy as np

D_MODEL = 2048
BATCH = 2
SEQ = 4096
DEPTH = 4
DEC_BATCH = 32
DEC_SEQ = 8
PAST_LEN = 16384
PAGE_SIZE = 128

MIX_W = D_MODEL // 2
N_BRANCH = 3
GLA_HEADS = 4
GLA_DK = MIX_W // 2 // GLA_HEADS
GLA_DV = MIX_W // GLA_HEADS
GLA_RANK = 16
GLA_TAU = 16.0
GLA_CHUNK = 64
SWA_HEADS = 16
SWA_KV = 2
SWA_HD = MIX_W // SWA_HEADS
SWA_GROUP = SWA_HEADS // SWA_KV
WINDOW = 128
SWA_BLOCK = 128
N_BUCKETS = 32
MAX_DIST = 128
CONV_W = 3
N_MEM = 256
X_HEADS = 4
X_HD = 128
D_FF = ((8 * D_MODEL // 3 + 127) // 128) * 128
EPS = 1e-6
SPLIT_SIZES = (GLA_HEADS * GLA_DK, GLA_HEADS * GLA_DK, MIX_W, MIX_W, GLA_RANK,
               SWA_HEADS * SWA_HD, SWA_KV * SWA_HD, SWA_KV * SWA_HD,
               MIX_W, MIX_W, MIX_W, N_BRANCH * D_MODEL)
N_IN = sum(SPLIT_SIZES)

kernel_name = 'hybrid_gla_swa_shortconv_decoder_step'


def rmsnorm(x, g):
    xf = x.astype(jnp.float32)
    y = xf * lax.rsqrt(jnp.mean(xf * xf, axis=-1, keepdims=True) + EPS)
    return (y * g.astype(jnp.float32)).astype(x.dtype)


def split_columns(z):
    idx, acc = [], 0
    for s in SPLIT_SIZES[:-1]:
        acc += s
        idx.append(acc)
    return jnp.split(z, idx, axis=-1)


def causal_dwconv(u, prev, w):
    T = u.shape[1]
    up = jnp.concatenate([prev.astype(u.dtype), u], axis=1)
    out = up[:, 0:T] * w[0]
    for j in range(1, CONV_W):
        out = out + up[:, j:j + T] * w[j]
    return out, up[:, T:]


def t5_bucket(dist):
    n = jnp.maximum(dist, 0)
    max_exact = N_BUCKETS // 2
    nf = jnp.maximum(n, 1).astype(jnp.float32)
    large = max_exact + (jnp.log(nf / max_exact) / math.log(MAX_DIST / max_exact)
                         * (N_BUCKETS - max_exact)).astype(jnp.int32)
    large = jnp.minimum(large, N_BUCKETS - 1)
    return jnp.where(n < max_exact, n, large)


def gla_chunked(q, k, v, lg, s0):
    B, T, H, _ = q.shape
    L = min(GLA_CHUNK, T)
    nc = -(-T // L)
    pad = nc * L - T

    def chunks(a):
        a = a.astype(jnp.float32)
        if pad:
            a = jnp.pad(a, ((0, 0), (0, pad), (0, 0), (0, 0)))
        return a.reshape(B, nc, L, H, a.shape[-1]).transpose(1, 0, 3, 2, 4)

    causal = jnp.tril(jnp.ones((L, L), dtype=bool))[:, :, None]

    def step(S, inp):
        qc, kc, vc, gc = inp
        b = jnp.cumsum(gc, axis=2)
        o = jnp.einsum('bhtd,bhdv->bhtv', qc * jnp.exp(b), S)
        decay = jnp.exp(jnp.where(causal, b[:, :, :, None, :] - b[:, :, None, :, :], -jnp.inf))
        a = jnp.einsum('bhtd,bhsd,bhtsd->bhts', qc, kc, decay)
        o = o + jnp.einsum('bhts,bhsv->bhtv', a, vc)
        b_end = b[:, :, -1:, :]
        S = (jnp.exp(b_end[:, :, 0, :, None]) * S
             + jnp.einsum('bhsd,bhsv->bhdv', kc * jnp.exp(b_end - b), vc))
        return S, o

    S, o = lax.scan(step, s0.astype(jnp.float32), (chunks(q), chunks(k), chunks(v), chunks(lg)))
    o = o.transpose(1, 0, 3, 2, 4).reshape(B, nc * L, H, -1)[:, :T]
    return o.astype(v.dtype), S.astype(s0.dtype)


def sink_attention(q, k, v, dist, valid, sinks, rel_bias):
    s = jnp.einsum('bnqkgd,bnskd->bnkgqs', q, k).astype(jnp.float32) * (SWA_HD ** -0.5)
    bias = rel_bias[t5_bucket(dist)].astype(jnp.float32)
    bias = bias.reshape(dist.shape + (SWA_KV, SWA_GROUP)).transpose(2, 3, 0, 1)
    s = jnp.where(valid[None, :, None, None], s + bias, -jnp.inf)
    sink = sinks.astype(jnp.float32).reshape(SWA_KV, SWA_GROUP)[:, :, None, None]
    m = jnp.maximum(jnp.max(s, axis=-1, keepdims=True), sink)
    p = jnp.exp(s - m)
    p = p / (jnp.sum(p, axis=-1, keepdims=True) + jnp.exp(sink - m))
    return jnp.einsum('bnkgqs,bnskd->bnqkgd', p.astype(v.dtype), v)


def swa_prompt(q, k, v, sinks, rel_bias):
    B, T = q.shape[:2]
    nb = T // SWA_BLOCK
    qb = q.reshape(B, nb, SWA_BLOCK, SWA_KV, SWA_GROUP, SWA_HD)

    def band(a):
        prev = jnp.concatenate([jnp.zeros_like(a[:, :SWA_BLOCK]), a[:, :T - SWA_BLOCK]], axis=1)
        return jnp.concatenate([prev.reshape(B, nb, SWA_BLOCK, SWA_KV, SWA_HD),
                                a.reshape(B, nb, SWA_BLOCK, SWA_KV, SWA_HD)], axis=2)

    i = jnp.arange(SWA_BLOCK)[:, None]
    j = jnp.arange(2 * SWA_BLOCK)[None, :]
    dist = SWA_BLOCK + i - j
    kpos = (jnp.arange(nb)[:, None, None] - 1) * SWA_BLOCK + j[None]
    valid = ((dist >= 0) & (dist <= WINDOW))[None] & (kpos >= 0)
    o = sink_attention(qb, band(k), band(v), dist, valid, sinks, rel_bias)
    return o.reshape(B, T, MIX_W), k[:, T - WINDOW:], v[:, T - WINDOW:]


def swa_sample(q, k, v, k_buf, v_buf, sinks, rel_bias):
    B, T = q.shape[:2]
    W = k_buf.shape[1]
    kc = jnp.concatenate([k_buf.astype(k.dtype), k], axis=1)
    vc = jnp.concatenate([v_buf.astype(v.dtype), v], axis=1)
    dist = W + jnp.arange(T)[:, None] - jnp.arange(W + T)[None, :]
    valid = ((dist >= 0) & (dist <= WINDOW))[None]
    o = sink_attention(q.reshape(B, 1, T, SWA_KV, SWA_GROUP, SWA_HD), kc[:, None], vc[:, None],
                       dist, valid, sinks, rel_bias)
    return o.reshape(B, T, MIX_W), kc[:, T:], vc[:, T:]


def parallel_mixers(h, w, rel_bias, gla_s0, swa_buf, conv_prev):
    B, T, _ = h.shape
    gq, gk, gv, gr, glr, sq, sk, sv, cb, cc, ch, gates = split_columns(h @ w['w_in'])
    lg = jax.nn.log_sigmoid((glr @ w['gla_gate_up'] + w['gla_gate_b']).astype(jnp.float32)) / GLA_TAU
    o, gla_S = gla_chunked(gq.reshape(B, T, GLA_HEADS, GLA_DK) * (GLA_DK ** -0.5),
                           gk.reshape(B, T, GLA_HEADS, GLA_DK),
                           gv.reshape(B, T, GLA_HEADS, GLA_DV),
                           lg.reshape(B, T, GLA_HEADS, GLA_DK), gla_s0)
    br_a = rmsnorm(o, w['gla_norm']).reshape(B, T, MIX_W) * jax.nn.silu(gr)
    sq = sq.reshape(B, T, SWA_KV, SWA_GROUP, SWA_HD)
    sk = sk.reshape(B, T, SWA_KV, SWA_HD)
    sv = sv.reshape(B, T, SWA_KV, SWA_HD)
    if swa_buf is None:
        br_b, kb, vb = swa_prompt(sq, sk, sv, w['swa_sinks'], rel_bias)
    else:
        br_b, kb, vb = swa_sample(sq, sk, sv, swa_buf[0], swa_buf[1], w['swa_sinks'], rel_bias)
    u, conv_buf = causal_dwconv(cc * ch, conv_prev, w['conv_w'])
    br_c = cb * u
    g = jax.nn.sigmoid(gates).reshape(B, T, N_BRANCH, D_MODEL)
    wb = w['w_branch']
    merged = (g[:, :, 0] * (br_a @ wb[0]) + g[:, :, 1] * (br_b @ wb[1]) + g[:, :, 2] * (br_c @ wb[2]))
    return merged @ w['w_out'], (gla_S, kb, vb, conv_buf)


def cross_attention(h, mem_k, mem_v, wq, wo):
    B, T, _ = h.shape
    q = (h @ wq).reshape(B, T, X_HEADS, X_HD)
    s = jnp.einsum('bthd,bmhd->bhtm', q, mem_k.astype(q.dtype)).astype(jnp.float32) * (X_HD ** -0.5)
    p = jax.nn.softmax(s, axis=-1).astype(h.dtype)
    o = jnp.einsum('bhtm,bmhd->bthd', p, mem_v.astype(h.dtype)).reshape(B, T, X_HEADS * X_HD)
    return o @ wo


def conv_ffn(h, w_up, cw, cbias, w_down, prev):
    u, g = jnp.split(h @ w_up, 2, axis=-1)
    gc, buf = causal_dwconv(g, prev, cw)
    return (jax.nn.silu(gc + cbias) * u) @ w_down, buf


def trunk_layer(x, w, rel_bias, gla_s0, swa_buf, conv_prev, ffn_prev, mem_k, mem_v):
    mix, (gla_S, kb, vb, conv_buf) = parallel_mixers(rmsnorm(x, w['norm_mix']), w, rel_bias,
                                                      gla_s0, swa_buf, conv_prev)
    x = x + mix
    x = x + cross_attention(rmsnorm(x, w['norm_x']), mem_k, mem_v, w['wx_q'], w['wx_o'])
    f, ffn_buf = conv_ffn(rmsnorm(x, w['norm_ffn']), w['ffn_up'], w['ffn_conv_w'], w['ffn_conv_b'],
                          w['ffn_down'], ffn_prev)
    return x + f, (gla_S, kb, vb, conv_buf, ffn_buf)


def setup_inputs(seed: int = 0) -> dict:
    key = jax.random.key(seed)
    ks = jax.random.split(key, 40)
    f32 = jnp.float32

    def nrm(k, shape, scale):
        return jax.random.normal(k, shape, f32) * scale

    def gain(k, shape):
        return 1.0 + 0.01 * jax.random.normal(k, shape, f32)

    W = min(WINDOW, PAST_LEN)
    return {
        'x_prompt': nrm(ks[0], (BATCH, SEQ, D_MODEL), 1.0),
        'x_sample': nrm(ks[1], (DEC_BATCH, DEC_SEQ, D_MODEL), 1.0),
        'state_gla': nrm(ks[2], (DEPTH, DEC_BATCH, GLA_HEADS, GLA_DK, GLA_DV), 1.0),
        'cache_swa_k': nrm(ks[3], (DEPTH, DEC_BATCH, W, SWA_KV, SWA_HD), 1.0),
        'cache_swa_v': nrm(ks[4], (DEPTH, DEC_BATCH, W, SWA_KV, SWA_HD), 1.0),
        'state_conv': nrm(ks[5], (DEPTH, DEC_BATCH, CONV_W - 1, MIX_W), 1.0),
        'state_ffn': nrm(ks[6], (DEPTH, DEC_BATCH, CONV_W - 1, D_FF), 1.0),
        'cache_mem_k': nrm(ks[7], (DEPTH, DEC_BATCH, N_MEM, X_HEADS, X_HD), 1.0),
        'cache_mem_v': nrm(ks[8], (DEPTH, DEC_BATCH, N_MEM, X_HEADS, X_HD), 1.0),
        'mem_prompt': nrm(ks[9], (BATCH, N_MEM, D_MODEL), 1.0),
        'norm_mix': gain(ks[10], (DEPTH, D_MODEL)),
        'w_in': nrm(ks[11], (DEPTH, D_MODEL, N_IN), D_MODEL ** -0.5),
        'gla_gate_up': nrm(ks[12], (DEPTH, GLA_RANK, GLA_HEADS * GLA_DK), GLA_RANK ** -0.5),
        'gla_gate_b': nrm(ks[13], (DEPTH, GLA_HEADS * GLA_DK), 0.1),
        'gla_norm': gain(ks[14], (DEPTH, GLA_DV)),
        'swa_sinks': nrm(ks[15], (DEPTH, SWA_HEADS), 0.5),
        'rel_bias': nrm(ks[16], (N_BUCKETS, SWA_HEADS), 0.5),
        'conv_w': nrm(ks[17], (DEPTH, CONV_W, MIX_W), CONV_W ** -0.5),
        'w_branch': nrm(ks[18], (DEPTH, N_BRANCH, MIX_W, D_MODEL), MIX_W ** -0.5),
        'w_out': nrm(ks[19], (DEPTH, D_MODEL, D_MODEL), D_MODEL ** -0.5),
        'norm_x': gain(ks[20], (DEPTH, D_MODEL)),
        'wx_q': nrm(ks[21], (DEPTH, D_MODEL, X_HEADS * X_HD), D_MODEL ** -0.5),
        'wx_k': nrm(ks[22], (DEPTH, D_MODEL, X_HEADS * X_HD), D_MODEL ** -0.5),
        'wx_v': nrm(ks[23], (DEPTH, D_MODEL, X_HEADS * X_HD), D_MODEL ** -0.5),
        'wx_o': nrm(ks[24], (DEPTH, X_HEADS * X_HD, D_MODEL), (X_HEADS * X_HD) ** -0.5),
        'norm_ffn': gain(ks[25], (DEPTH, D_MODEL)),
        'ffn_up': nrm(ks[26], (DEPTH, D_MODEL, 2 * D_FF), D_MODEL ** -0.5),
        'ffn_conv_w': nrm(ks[27], (DEPTH, CONV_W, D_FF), CONV_W ** -0.5),
        'ffn_conv_b': nrm(ks[28], (DEPTH, D_FF), 0.01),
        'ffn_down': nrm(ks[29], (DEPTH, D_FF, D_MODEL), D_FF ** -0.5),
        'norm_final': gain(ks[30], (D_MODEL,)),
    }


def reference(x_prompt, x_sample, state_gla, cache_swa_k, cache_swa_v, state_conv, state_ffn,
              cache_mem_k, cache_mem_v, mem_prompt, norm_mix, w_in, gla_gate_up, gla_gate_b,
              gla_norm, swa_sinks, rel_bias, conv_w, w_branch, w_out, norm_x, wx_q, wx_k, wx_v,
              wx_o, norm_ffn, ffn_up, ffn_conv_w, ffn_conv_b, ffn_down, norm_final):
    Bp = x_prompt.shape[0]
    yp, ys = x_prompt, x_sample
    st_p, st_s = [], []
    for i in range(DEPTH):
        w = {'norm_mix': norm_mix[i], 'w_in': w_in[i], 'gla_gate_up': gla_gate_up[i],
             'gla_gate_b': gla_gate_b[i], 'gla_norm': gla_norm[i], 'swa_sinks': swa_sinks[i],
             'conv_w': conv_w[i], 'w_branch': w_branch[i], 'w_out': w_out[i], 'norm_x': norm_x[i],
             'wx_q': wx_q[i], 'wx_o': wx_o[i], 'norm_ffn': norm_ffn[i], 'ffn_up': ffn_up[i],
             'ffn_conv_w': ffn_conv_w[i], 'ffn_conv_b': ffn_conv_b[i], 'ffn_down': ffn_down[i]}
        mem_k = (mem_prompt @ wx_k[i]).reshape(Bp, -1, X_HEADS, X_HD)
        mem_v = (mem_prompt @ wx_v[i]).reshape(Bp, -1, X_HEADS, X_HD)
        zero_gla = jnp.zeros((Bp, GLA_HEADS, GLA_DK, GLA_DV), jnp.float32)
        zero_conv = jnp.zeros((Bp, CONV_W - 1, MIX_W), yp.dtype)
        zero_ffn = jnp.zeros((Bp, CONV_W - 1, D_FF), yp.dtype)
        yp, sp = trunk_layer(yp, w, rel_bias, zero_gla, None, zero_conv, zero_ffn, mem_k, mem_v)
        ys, ss = trunk_layer(ys, w, rel_bias, state_gla[i], (cache_swa_k[i], cache_swa_v[i]),
                             state_conv[i], state_ffn[i], cache_mem_k[i], cache_mem_v[i])
        st_p.append(sp + (mem_k, mem_v))
        st_s.append(ss)
    y_prompt = rmsnorm(yp, norm_final)
    y_sample = rmsnorm(ys, norm_final)

    def stack(lst, j):
        return jnp.stack([s[j] for s in lst])

    return (y_prompt, y_sample, stack(st_p, 0), stack(st_s, 0), stack(st_p, 1), stack(st_p, 2),
            stack(st_s, 1), stack(st_s, 2), stack(st_p, 3), stack(st_s, 3), stack(st_p, 4),
            stack(st_s, 4), stack(st_p, 5), stack(st_p, 6))
```

```python
import math
import numpy as np
from contextlib import ExitStack
import concourse.bass as bass
import concourse.mybir as mybir
from concourse.bass_utils import run_bass_kernel_spmd

F32 = mybir.dt.float32
BF16 = mybir.dt.bfloat16
AF = mybir.ActivationFunctionType
ALU = mybir.AluOpType

D = 2048
MIX = 1024
DFF = 5504
NIN = 13584
C_GQ, C_GK, C_GV, C_GR, C_GLR, C_SQ, C_SK, C_SV, C_CB, C_CC, C_CH, C_GATE = (
    0, 512, 1024, 2048, 3072, 3088, 4112, 4240, 4368, 5392, 6416, 7440)
NEG = -30000.0
ENGS = ("pe", "act", "dve", "pool", "sp")


class Op:
    __slots__ = ("eng", "fn", "seq", "needed", "sig", "is_dma", "dsem", "dval", "deps")

    def __init__(self, eng, fn):
        self.eng = eng
        self.fn = fn
        self.deps = []
        self.needed = False
        self.sig = None
        self.is_dma = False
        self.dsem = None
        self.dval = 0


class Sched:
    def __init__(self, nc, stack):
        self.nc = nc
        self.stack = stack
        self.planning = False
        self.reset()
        self.sem = {e: stack.enter_context(nc.semaphore("s_" + e)) for e in ENGS}
        self.dma_sems = {}

    def reset(self):
        self.ops = {e: [] for e in ENGS}
        self.last_w = {}
        self.readers = {}
        self.waited = {e: {} for e in ENGS}
        self.dma_cnt = {}
        self.fake_w = set()

    def _dep(self, op, p):
        if p is None or p is op:
            return
        w = self.waited[op.eng]
        if p.is_dma:
            k = ("d", id(p.dsem))
            if w.get(k, -1) >= p.dval:
                return
            w[k] = p.dval
            op.deps.append(p)
            return
        if p.eng == op.eng and op.eng == "pe" and not op.is_dma:
            return
        if w.get(p.eng, -1) >= p.seq:
            return
        w[p.eng] = p.seq
        p.needed = True
        op.deps.append(p)

    def add(self, eng, fn, reads=(), writes=(), dma_stream=None):
        if self.planning:
            return None
        op = Op(eng, fn)
        op.seq = len(self.ops[eng])
        if dma_stream is not None:
            op.is_dma = True
            if dma_stream not in self.dma_sems:
                self.dma_sems[dma_stream] = self.stack.enter_context(self.nc.semaphore("d_" + str(dma_stream)))
            self.dma_cnt[dma_stream] = self.dma_cnt.get(dma_stream, 0) + 16
            op.dsem = self.dma_sems[dma_stream]
            op.dval = self.dma_cnt[dma_stream]
        psr = [k for k in reads if isinstance(k, str) and k.startswith("ps")]
        for k in reads:
            lw = self.last_w.get(k)
            if lw is not None and k in psr and (id(lw), k) in self.fake_w and lw.eng == eng:
                continue
            self._dep(op, lw)
        for k in writes:
            self._dep(op, self.last_w.get(k))
            rd = self.readers.get(k)
            if rd:
                for r in rd.values():
                    self._dep(op, r)
        rk = eng if not op.is_dma else ("dma", dma_stream)
        for k in reads:
            self.readers.setdefault(k, {})[rk] = op
        for k in writes:
            self.last_w[k] = op
            self.readers[k] = {}
        for k in psr:
            if k not in writes:
                self.last_w[k] = op
                self.fake_w.add((id(op), k))
        self.ops[eng].append(op)
        return op

    def finish(self, final_waits=()):
        nc = self.nc
        for e in ENGS:
            c = 0
            for op in self.ops[e]:
                if op.needed and not op.is_dma:
                    c += 1
                    op.sig = c
        fin = Op("sp", None)
        seen = set()
        for p in final_waits:
            if p is not None and id(p) not in seen:
                seen.add(id(p))
                fin.deps.append(p)
        sems = self.sem
        ops = self.ops

        def emit_engine(e, eng):
            for op in ops[e] + ([fin] if e == "sp" else []):
                for p in op.deps:
                    if p.is_dma:
                        eng.wait_ge(p.dsem, p.dval)
                    else:
                        eng.wait_ge(sems[p.eng], p.sig)
                if op.fn is None:
                    continue
                ins = op.fn(eng)
                if op.is_dma:
                    ins.then_inc(op.dsem, 16)
                elif op.needed:
                    ins.then_inc(sems[e], 1)

        with nc.Block() as block:
            @block.tensor
            def _(eng):
                emit_engine("pe", eng)

            @block.scalar
            def _(eng):
                emit_engine("act", eng)

            @block.vector
            def _(eng):
                emit_engine("dve", eng)

            @block.gpsimd
            def _(eng):
                emit_engine("pool", eng)

            @block.sync
            def _(eng):
                emit_engine("sp", eng)


def t5_bucket_np(dist):
    n = np.maximum(dist, 0)
    nf = np.maximum(n, 1).astype(np.float32)
    large = 16 + (np.log(nf / np.float32(16)) / np.float32(math.log(128 / 16)) * np.float32(16)).astype(np.int32)
    large = np.minimum(large, 31)
    return np.where(n < 16, n, large)


def host_consts():
    c = {}
    c["c_ident"] = np.eye(128, dtype=np.float32)
    s = np.arange(128)[:, None]
    t = np.arange(128)[None, :]
    c["c_triS"] = np.where(s <= t, -1.0 / 16, 0.0).astype(np.float32)
    c["c_triU"] = np.where(s > t, -1.0 / 16, 0.0).astype(np.float32)
    c["c_mask"] = (s <= t).astype(np.float32)
    c["c_J"] = np.eye(128, dtype=np.float32)[::-1].copy()
    n = np.arange(256)
    u = n - 127
    oh = np.zeros((32, 512), np.float32)
    ng = np.zeros((16, 512), np.float32)
    dcur = u
    vcur = (dcur >= 0) & (dcur <= 128)
    bc = t5_bucket_np(np.clip(dcur, 0, 128))
    dprev = 128 + u
    vprev = (dprev >= 0) & (dprev <= 128)
    bp = t5_bucket_np(np.clip(dprev, 0, 128))
    for i in range(256):
        if vcur[i]:
            oh[bc[i], i] = 1.0
        else:
            ng[:, i] = NEG
        if vprev[i]:
            oh[bp[i], 256 + i] = 1.0
        else:
            ng[:, 256 + i] = NEG
    c["c_oh"] = oh
    c["c_ng"] = ng
    return c


import os
STAGE = int(os.environ.get("KSTAGE", "99"))


def build(SEQ, DEPTH, NSQ=4, TS=8):
    nc = bass.Bass("TRN2", target_bir_lowering=False)
    NPT = SEQ // 512
    NST = NSQ * TS

    def din(name, shape):
        return nc.dram_tensor(name, list(shape), F32, kind="ExternalInput").ap()

    def dout(name, shape):
        return nc.dram_tensor(name, list(shape), F32, kind="ExternalOutput").ap()

    xp = din("xp", [SEQ, D])
    xs = din("xs", [NST, D])
    sgla = din("sgla", [DEPTH, NSQ, 4, 128, 256])
    csk = din("csk", [DEPTH, NSQ, 128, 128])
    csv = din("csv", [DEPTH, NSQ, 128, 128])
    sconv = din("sconv", [DEPTH, NSQ, 2, MIX])
    sffn = din("sffn", [DEPTH, NSQ, 2, DFF])
    cmk = din("cmk", [DEPTH, NSQ, 256, 512])
    cmv = din("cmv", [DEPTH, NSQ, 256, 512])
    memp = din("memp", [256, D])
    norm_mix = din("norm_mix", [DEPTH, D])
    w_in = din("w_in", [DEPTH, D, NIN])
    gate_up = din("gla_gate_up", [DEPTH, 16, 512])
    gate_b = din("gla_gate_b", [DEPTH, 512])
    gla_norm = din("gla_norm", [DEPTH, 256])
    swa_sinks = din("swa_sinks", [DEPTH, 16])
    rel_bias = din("rel_bias", [32, 16])
    conv_w = din("conv_w", [DEPTH, 3, MIX])
    w_branch = din("w_branch", [DEPTH, 3, MIX, D])
    w_out = din("w_out", [DEPTH, D, D])
    norm_x = din("norm_x", [DEPTH, D])
    wx_q = din("wx_q", [DEPTH, D, 512])
    wx_k = din("wx_k", [DEPTH, D, 512])
    wx_v = din("wx_v", [DEPTH, D, 512])
    wx_o = din("wx_o", [DEPTH, 512, D])
    norm_ffn = din("norm_ffn", [DEPTH, D])
    ffn_up = din("ffn_up", [DEPTH, D, 2 * DFF])
    ffn_cw = din("ffn_conv_w", [DEPTH, 3, DFF])
    ffn_cb = din("ffn_conv_b", [DEPTH, DFF])
    ffn_down = din("ffn_down", [DEPTH, DFF, D])
    norm_final = din("norm_final", [D])
    c_ident = din("c_ident", [128, 128])
    c_triS = din("c_triS", [128, 128])
    c_triU = din("c_triU", [128, 128])
    c_mask = din("c_mask", [128, 128])
    c_J = din("c_J", [128, 128])
    c_oh = din("c_oh", [32, 512])
    c_ng = din("c_ng", [16, 512])

    yp = dout("yp", [SEQ, D])
    ys = dout("ys", [NST, D])
    glap = dout("glap", [DEPTH, 4, 128, 256])
    glas = dout("glas", [DEPTH, NSQ, 4, 128, 256])
    swakp = dout("swakp", [DEPTH, 128, 128])
    swavp = dout("swavp", [DEPTH, 128, 128])
    swaks = dout("swaks", [DEPTH, NSQ, 128, 128])
    swavs = dout("swavs", [DEPTH, NSQ, 128, 128])
    convp = dout("convp", [DEPTH, 2, MIX])
    convs = dout("convs", [DEPTH, NSQ, 2, MIX])
    ffnp = dout("ffnp", [DEPTH, 2, DFF])
    ffns = dout("ffns", [DEPTH, NSQ, 2, DFF])
    memkp = dout("memkp", [DEPTH, 256, 512])
    memvp = dout("memvp", [DEPTH, 256, 512])

    xd = nc.dram_tensor("xd", [SEQ + NST, D], F32, kind="Internal").ap()
    Rd_t = nc.dram_tensor("Rd", [16, 512], F32, kind="Internal")
    Rd = Rd_t.ap()

    finals = []
    with ExitStack() as st:
        S = Sched(nc, st)

        def sb(name, shape, dt=F32):
            return st.enter_context(nc.sbuf_tensor(name, list(shape), dt))

        identf = sb("identf", [128, 128])
        identb = sb("identb", [128, 128], BF16)
        onesb = sb("onesb", [128, 128], BF16)
        onesf = sb("onesf", [1, 128])
        triS = sb("triS", [128, 128])
        triU = sb("triU", [128, 128])
        mask4 = sb("mask4", [128, 4, 128])
        Jt = sb("Jt", [128, 128])
        biasC = sb("biasC", [128, 16, 128])
        biasP = sb("biasP", [128, 16, 128])
        pv = sb("pv", [128, 256])
        gfin = sb("gfin", [128, 16])
        esink = sb("esink", [128, 16])
        gup = sb("gup", [17, 512])
        xres = sb("xres", [128, 4, D])
        hT = sb("hT", [128, 16, 512], BF16)
        h0 = sb("h0", [128, D], BF16)
        nstat = sb("nstat", [128, 8])
        wsl = [sb("wsl%d" % i, [128, 16, 512], BF16) for i in range(2)]
        ABF = sb("ABF", [128, 17408], BF16)
        AFF = sb("AFF", [128, 4736], F32)

        def _view(t, np_, off, shape):
            n = 1
            for d in shape[1:]:
                n *= d
            v = t[0:np_, off:off + n]
            if len(shape) == 3:
                v = v.rearrange("p (a b) -> p a b", a=shape[1])
            elif len(shape) == 4:
                v = v.rearrange("p (a b c) -> p a b c", a=shape[1], b=shape[2])
            return v

        def vb(off, shape):
            return _view(ABF, shape[0], off, shape)

        def vf(off, shape):
            return _view(AFF, shape[0], off, shape)

        qT = vb(0, [128, 4, 512]); kT = vb(2048, [128, 4, 512]); ktok = vb(4096, [128, 4, 512])
        vtok = vb(6144, [128, 4, 1024]); grT = vb(10240, [128, 8, 512])
        qe = vb(14336, [128, 4, 128]); ke = vb(14848, [128, 4, 128]); kd = vb(15360, [128, 512])
        ATm = vb(15872, [128, 4, 128]); osq = vb(16384, [128, 8, 128])
        glr = vf(0, [17, 512]); l1 = vf(512, [128, 512]); E1 = vf(1024, [128, 4, 128]); E2 = vf(1536, [128, 4, 128])
        E3 = vf(2048, [128, 512]); grs = vf(2560, [128, 4, 128])
        sqT = vb(0, [128, 8, 512]); kdup = vb(4096, [128, 2, 128]); PTp = vb(4352, [128, 4, 128]); PTc = vb(4864, [128, 4, 128])
        skv = vf(0, [128, 4, 256]); stg = vf(1024, [128, 2, 512]); sbias = vf(2048, [128, 512]); rden = vf(2560, [128, 4, 128])
        cbT = vb(0, [128, 8, 512]); ccT = vb(4096, [128, 8, 512])
        upad = vf(0, [128, 8, 528]); ctmp = vf(4224, [128, 512])
        qxT = vb(0, [128, 4, 512]); oxT = vb(2048, [128, 4, 512]); PTx = vb(4096, [128, 2, 128]); vtokx = vb(4352, [128, 2, 512])
        stgx = vf(0, [128, 2, 512]); rdenx = vf(1024, [128, 128])
        gpad = vf(0, [128, 528]); ctmp5 = vf(528, [128, 512])
        ystage = vf(0, [128, D])
        GK = {
            1: ["qT", "kT", "ktok", "vtok", "grT", "qe", "ke", "kd", "ATm", "osq", "glr", "l1", "E1", "E2", "E3", "grs"],
            2: ["sqT", "kdup", "PTp", "PTc", "skv", "stg", "sbias", "rden"],
            3: ["cbT", "ccT", "upad", "ctmp"],
            4: ["qxT", "oxT", "PTx", "vtokx", "stgx", "rdenx"],
            5: ["gpad", "ctmp5"],
            6: ["ystage"],
        }

        def enter_group(g):
            if S.planning:
                return
            merged = {}
            for og, keys in GK.items():
                if og == g:
                    continue
                for ok in keys:
                    for rk, op in S.readers.get(ok, {}).items():
                        merged[(ok, rk)] = op
                    lw = S.last_w.get(ok)
                    if lw is not None:
                        merged[(ok, "w")] = lw
            for nk in GK[g]:
                d = dict(S.readers.get(nk, {}))
                d.update(merged)
                S.readers[nk] = d

        Sst = sb("Sst", [128, 4, 256])
        Sbf = sb("Sbf", [128, 4, 256], BF16)
        KTd = sb("KTd", [128, 2, 5 * 128], BF16)
        Vd = sb("Vd", [128, 5, 2, 128], BF16)
        br = sb("br", [128, 8, 512], BF16)
        mg = sb("mg", [128, 16, 512], BF16)
        sig = sb("sig", [128, 512])
        tmpb = sb("tmpb", [128, 512], BF16)
        KxT = sb("KxT", [128, 4, 256], BF16)
        Vx = sb("Vx", [128, 2, 512], BF16)
        gtail = sb("gtail", [128, 43, 4, 2])
        ctail = sb("ctail", [128, 8, 4, 2])
        tl = sb("tl", [128, 128])
        tls = sb("tls", [128, 128])
        tlo = sb("tlo", [128, 128])

        psb = [st.enter_context(nc.psum_tensor("ps%d" % i, [128, 512], F32)) for i in range(7)]
        pstT = st.enter_context(nc.psum_tensor("pst", [128, 8, 128], BF16))
        ps_i = [0]

        def ps():
            i = ps_i[0] % 7
            ps_i[0] += 1
            return psb[i], "ps%d" % i

        def MM(out, lhsT, rhs, start, stop, r, w):
            return S.add("pe", lambda e: e.matmul(out, lhsT=lhsT, rhs=rhs, start=start, stop=stop), r, w)

        def TR(out, in_, idt, r, w):
            return S.add("pe", lambda e: e.transpose(out=out, in_=in_, identity=idt), r, w)

        def ACT(out, in_, func, r, w, scale=None, bias=None, accum=None):
            kw = {}
            if scale is not None:
                kw["scale"] = scale
            if bias is not None:
                kw["bias"] = bias
            if accum is not None:
                kw["accum_out"] = accum
            return S.add("act", lambda e: e.activation(out=out, in_=in_, func=func, **kw), r, w)

        def TS_(eng, out, in0, s1, s2, op0, op1, r, w):
            if op1 is None:
                return S.add(eng, lambda e: e.tensor_scalar(out=out, in0=in0, scalar1=s1, scalar2=None, op0=op0), r, w)
            return S.add(eng, lambda e: e.tensor_scalar(out=out, in0=in0, scalar1=s1, scalar2=s2, op0=op0, op1=op1), r, w)

        def TT(eng, out, in0, in1, op, r, w):
            return S.add(eng, lambda e: e.tensor_tensor(out=out, in0=in0, in1=in1, op=op), r, w)

        def STT(eng, out, in0, sc, in1, op0, op1, r, w):
            return S.add(eng, lambda e: e.scalar_tensor_tensor(out=out, in0=in0, scalar=sc, in1=in1, op0=op0, op1=op1), r, w)

        def CP(eng, out, in_, r, w):
            if eng == "act":
                return S.add("act", lambda e: e.copy(out=out, in_=in_), r, w)
            return S.add(eng, lambda e: e.tensor_copy(out=out, in_=in_), r, w)

        def MS(eng, out, val, w):
            return S.add(eng, lambda e: e.memset(out, val), (), w)

        def DMA(q, out, in_, r, w, stream, slow=False):
            if slow:
                return S.add(q, lambda e: e.dma_start(out=out, in_=in_, allow_slow_non_contiguous=True), r, w, dma_stream=stream)
            return S.add(q, lambda e: e.dma_start(out=out, in_=in_), r, w, dma_stream=stream)

        class WS:
            plan = []
            pos = 0
            issued = 0

        def wget(src2d, r0, nkc, c0, ncols):
            spec = (src2d, r0, nkc, c0, ncols)
            if S.planning:
                WS.plan.append(spec)
                return wsl[0][:, 0:nkc, 0:ncols], "w0"
            i = WS.pos
            WS.pos += 1
            while WS.issued < min(len(WS.plan), i + 2):
                j = WS.issued
                s2, rr, nk, cc, ncl = WS.plan[j]
                slot = j % 2
                src = s2[rr * 128:(rr + nk) * 128, cc:cc + ncl].rearrange("(c p) n -> p c n", p=128)
                DMA("pool", wsl[slot][:, 0:nk, 0:ncl], src, (), ["w%d" % slot], "w%d" % slot)
                WS.issued += 1
            return wsl[i % 2][:, 0:nkc, 0:ncols], "w%d" % (i % 2)

        def setup():
            DMA("sp", identf[:], c_ident, (), ["identf"], "c0")
            DMA("sp", triS[:], c_triS, (), ["triS"], "c1")
            DMA("sp", triU[:], c_triU, (), ["triU"], "c2")
            DMA("sp", Jt[:], c_J, (), ["Jt"], "c3")
            for h in range(4):
                DMA("sp", mask4[:, h, :], c_mask, (), ["mask4"], "c4")
            CP("dve", identb[:], identf[:], ["identf"], ["identb"])
            MS("dve", onesb[:], 1.0, ["onesb"])
            MS("dve", onesf[:], 1.0, ["onesf"])
            vec_rows([(0, norm_final.rearrange("(c f) -> c f", f=128))], gfin, "gfin")
            DMA("sp", stg[0:32, 0, :], c_oh, (), ["stg"], "stg")
            DMA("sp", sig[0:32, 0:16], rel_bias, (), ["sig"], "c6")
            DMA("sp", l1[0:16, :], c_ng, (), ["l1"], "c7")
            p, pk = ps()
            MM(p[0:16, :], sig[0:32, 0:16], stg[0:32, 0, :], True, True, ["sig", "stg"], [pk])
            TT("dve", E3[0:16, :], p[0:16, :], l1[0:16, :], ALU.add, [pk, "l1"], ["E3"])
            DMA("sp", Rd, E3[0:16, :], ["E3"], ["Rd"], "rd")
            for which, dst in ((0, biasC), (1, biasP)):
                src = bass.AP(Rd_t, which * 256, [[1, 128], [512, 16], [1, 128]])
                DMA("sp", xres[:, 0, 0:2048].rearrange("p (h i) -> p h i", h=16), src, ["Rd"], ["xres0"], "xl0", slow=True)
                for hh in range(4):
                    p, pk = ps()
                    MM(p[:], Jt[:], xres[:, 0, hh * 512:(hh + 1) * 512], True, True, ["Jt", "xres0"], [pk])
                    CP("dve", dst[:, hh * 4:(hh + 1) * 4, :], p[:].rearrange("p (h i) -> p h i", h=4), [pk], ["bias"])

        PV_NM, PV_NX, PV_NF, PV_GN, PV_CW, PV_FW, PV_FB = 0, 16, 32, 48, 50, 74, 203

        def vec_rows(items, dst, dkey):
            tot = max(r0 + ap_.shape[0] for r0, ap_ in items)
            for base in range(0, tot, 128):
                n = min(128, tot - base)
                for r0, ap_ in items:
                    nr = ap_.shape[0]
                    lo = max(r0, base)
                    hi = min(r0 + nr, base + n)
                    if lo < hi:
                        DMA("sp", tl[lo - base:hi - base, :], ap_[lo - r0:hi - r0, :], (), ["tl"], "tl")
                p, pk = ps()
                MM(p[:, 0:n], tl[0:n, :], identf[0:n, 0:n], True, True, ["tl", "identf"], [pk])
                CP("dve", dst[:, base:base + n], p[:, 0:n], [pk], [dkey])

        def layer_setup(l):
            q = "sp"
            r2 = lambda v: v.rearrange("(c f) -> c f", f=128)
            items = [(PV_NM, r2(norm_mix[l])), (PV_NX, r2(norm_x[l])), (PV_NF, r2(norm_ffn[l])), (PV_GN, r2(gla_norm[l])),
                     (PV_CW, conv_w[l].rearrange("j (c f) -> (j c) f", f=128)),
                     (PV_FW, ffn_cw[l].rearrange("j (c f) -> (j c) f", f=128)), (PV_FB, r2(ffn_cb[l]))]
            vec_rows(items, pv, "pv")
            DMA(q, tl[0:1, 0:16], swa_sinks[l:l + 1, :], (), ["tl"], "tl")
            p, pk = ps()
            MM(p[:, 0:16], onesf[0:1, :], tl[0:1, 0:16], True, True, ["onesf", "tl"], [pk])
            ACT(esink[:], p[:, 0:16], AF.Exp, [pk], ["esink"])
            DMA(q, gup[0:16, :], gate_up[l], (), ["gup"], "gup")
            DMA(q, gup[16:17, :], gate_b[l:l + 1, :], (), ["gup"], "gup")

        def norm_to_hT(Lq, nblk, gcol, final_out=None):
            for b in range(nblk):
                xk = "xres%d" % b
                MS("dve", nstat[:Lq, 0:1], 0.0, ["nstat"])
                ACT(h0[:Lq, :], xres[:Lq, b, :], AF.Square, [xk], ["h0", "nstat"], accum=nstat[:Lq, 0:1])
                ACT(nstat[:Lq, 1:2], nstat[:Lq, 0:1], AF.Ln, ["nstat"], ["nstat"], scale=1.0 / D, bias=1e-6)
                ACT(nstat[:Lq, 2:3], nstat[:Lq, 1:2], AF.Exp, ["nstat"], ["nstat"], scale=-0.5)
                TS_("dve", h0[:Lq, :], xres[:Lq, b, :], nstat[:Lq, 2:3], None, ALU.mult, None, [xk, "nstat"], ["h0"])
                for half in range(2):
                    for j in range(8):
                        c = half * 8 + j
                        TR(pstT[:, j, :Lq], h0[:Lq, c * 128:(c + 1) * 128], identb[:Lq, :Lq], ["h0", "identb"], ["pst"])
                    for j in range(8):
                        c = half * 8 + j
                        eng = "dve" if j % 2 == 0 else "pool"
                        if eng == "pool":
                            ACT(hT[:, c, b * Lq:(b + 1) * Lq], pstT[:, j, :Lq], AF.Copy, ["pst", "pv", "gfin"], ["hT"],
                                scale=gcol[:, c:c + 1])
                        else:
                            TS_("dve", hT[:, c, b * Lq:(b + 1) * Lq], pstT[:, j, :Lq], gcol[:, c:c + 1], None, ALU.mult, None,
                                ["pst", "pv", "gfin"], ["hT"])

        def final_norm_store(Lq, nblk, dst_rows):
            norm_to_hT(Lq, nblk, gfin)
            enter_group(6)
            for b in range(nblk):
                for half in range(4):
                    p, pk = ps()
                    for j in range(4):
                        c = half * 4 + j
                        MM(p[:Lq, j * 128:(j + 1) * 128], hT[:, c, b * Lq:(b + 1) * Lq], identb[:], True, True,
                           ["hT", "identb"], [pk])
                    CP("act" if half % 2 else "dve", ystage[:Lq, half * 512:(half + 1) * 512], p[:Lq, :], [pk], ["ystage"])
                finals.append(DMA("sp", dst_rows(b), ystage[:Lq, :], ["ystage"], (), "yst"))

        def proj_F(wsrc, c0, ncols, nkc, act_in, NT, evac, in_key):
            done = 0
            while done < ncols:
                ncl = min(512, ncols - done)
                wv, wk = wget(wsrc, 0, nkc, c0 + done, ncl)
                for o in range(ncl // 128):
                    p, pk = ps()
                    for kc in range(nkc):
                        MM(p[:, 0:NT], wv[:, kc, o * 128:(o + 1) * 128], act_in[:, kc, 0:NT], kc == 0, kc == nkc - 1,
                           [wk, in_key], [pk])
                    evac((done // 128) + o, p, pk)
                done += ncl

        def proj_T(wsrc, c0, ncols, nkc, act_in, Lq, nblk, evac, in_key, r0=0):
            done = 0
            while done < ncols:
                ncl = min(512, ncols - done)
                wv, wk = wget(wsrc, r0, nkc, c0 + done, ncl)
                for b in range(nblk):
                    p, pk = ps()
                    for kc in range(nkc):
                        MM(p[:Lq, 0:ncl], act_in[:, kc, b * Lq:(b + 1) * Lq], wv[:, kc, :], kc == 0, kc == nkc - 1,
                           [wk, in_key], [pk])
                    evac(b, done, ncl, p, pk)
                done += ncl

        def do_tile(l, kind, ti):
            sample = kind == "s"
            Lq = TS if sample else 128
            nblk = NSQ if sample else 4
            NT = Lq * nblk
            nseg = nblk if sample else 1
            sl = Lq if sample else 512
            first = (not sample) and ti == 0
            last = (not sample) and ti == NPT - 1
            row0 = SEQ if sample else ti * 512
            win = w_in[l]

            def xrow(b):
                return slice(row0 + b * Lq, row0 + (b + 1) * Lq)

            src = (xs if sample else xp) if l == 0 else xd
            for b in range(nblk):
                if l == 0:
                    rs = slice(b * Lq, (b + 1) * Lq) if sample else xrow(b)
                else:
                    rs = xrow(b)
                DMA("sp", xres[:Lq, b, :], src[rs, :], [("xd", kind, ti, b)], ["xres%d" % b], "xl%d" % b)

            norm_to_hT(Lq, nblk, pv[:, PV_NM:PV_NM + 16])

            if STAGE < 4:
                return
            enter_group(1)

            def ev_q(oc, p, pk):
                CP("act", qT[:, oc, 0:NT], p[:, 0:NT], [pk], ["qT"])
            proj_F(win, C_GQ, 512, 16, hT, NT, ev_q, "hT")

            def ev_k(oc, p, pk):
                CP("dve", kT[:, oc, 0:NT], p[:, 0:NT], [pk], ["kT"])
            proj_F(win, C_GK, 512, 16, hT, NT, ev_k, "hT")

            def ev_kt(b, c0, ncl, p, pk):
                CP("act", ktok[:Lq, b, :], p[:Lq, :], [pk], ["ktok"])
            proj_T(win, C_GK, 512, 16, hT, Lq, nblk, ev_kt, "hT")

            def ev_vt(b, c0, ncl, p, pk):
                CP("dve" if b % 2 else "act", vtok[:Lq, b, c0:c0 + ncl], p[:Lq, 0:ncl], [pk], ["vtok"])
            proj_T(win, C_GV, 1024, 16, hT, Lq, nblk, ev_vt, "hT")

            def ev_gr(oc, p, pk):
                ACT(grT[:, oc, 0:NT], p[:, 0:NT], AF.Silu, [pk], ["grT"])
            proj_F(win, C_GR, 1024, 16, hT, NT, ev_gr, "hT")

            wv, wk = wget(win, 0, 16, C_GLR, 16)
            p, pk = ps()
            for kc in range(16):
                MM(p[0:16, 0:NT], wv[:, kc, 0:16], hT[:, kc, 0:NT], kc == 0, kc == 15, [wk, "hT"], [pk])
            MS("dve", glr[:, 0:NT], 1.0, ["glr"])
            CP("dve", glr[0:16, 0:NT], p[0:16, 0:NT], [pk], ["glr"])

            if STAGE < 5:
                return
            if first:
                MS("dve", Sst[:], 0.0, ["Sst"])
                MS("pool", Sbf[:], 0.0, ["Sbf"])
            for b in range(nblk):
                tk = slice(b * Lq, (b + 1) * Lq)
                if sample:
                    DMA("sp", Sst[:], sgla[l, b].rearrange("h d v -> d h v"), (), ["Sst"], "sst")
                    CP("act", Sbf[:], Sst[:], ["Sst"], ["Sbf"])
                p, pk = ps()
                MM(p[:Lq, :], glr[:, tk], gup[:, :], True, True, ["glr", "gup"], [pk])
                ACT(l1[:Lq, :], p[:Lq, :], AF.Exp, [pk], ["l1"], scale=-1.0)
                ACT(l1[:Lq, :], l1[:Lq, :], AF.Ln, ["l1"], ["l1"], bias=1.0)
                pb, pbk = ps()
                for h in range(4):
                    MM(pb[:, h * Lq:(h + 1) * Lq], l1[:Lq, h * 128:(h + 1) * 128], triS[:Lq, :Lq], True, True,
                       ["l1", "triS"], [pbk])
                pu, puk = ps()
                MM(pu[:Lq, :], triU[:Lq, :Lq], l1[:Lq, :], True, True, ["l1", "triU"], [puk])
                pb3 = pb[:, 0:4 * Lq].rearrange("p (h t) -> p h t", h=4)
                ACT(E1[:, :, :Lq], pb3, AF.Exp, [pbk], ["E1"])
                ACT(E2[:, :, :Lq], pb3, AF.Exp, [pbk], ["E2"], scale=-1.0)
                ACT(E3[:Lq, :], pu[:Lq, :], AF.Exp, [puk], ["E3"])
                STT("dve", qe[:, :, :Lq], qT[:, :, tk], 128 ** -0.5, E1[:, :, :Lq], ALU.mult, ALU.mult, ["qT", "E1"], ["qe"])
                TT("dve", ke[:, :, :Lq], kT[:, :, tk], E2[:, :, :Lq], ALU.mult, ["kT", "E2"], ["ke"])
                TT("pool", kd[:Lq, :], ktok[:Lq, b, :], E3[:Lq, :], ALU.mult, ["ktok", "E3"], ["kd"])
                pa, pak = ps()
                for h in range(4):
                    MM(pa[:Lq, h * Lq:(h + 1) * Lq], ke[:, h, :Lq], qe[:, h, :Lq], True, True, ["ke", "qe"], [pak])
                TT("dve", ATm[:Lq, :, :Lq], pa[:Lq, 0:4 * Lq].rearrange("p (h t) -> p h t", h=4), mask4[:Lq, :, :Lq],
                   ALU.mult, [pak, "mask4"], ["ATm"])
                po = [ps(), ps()]
                for h in range(4):
                    for vc in range(2):
                        i8 = h * 2 + vc
                        pp, ppk = po[i8 // 4]
                        o_ = pp[:, (i8 % 4) * Lq:(i8 % 4 + 1) * Lq]
                        MM(o_, Sbf[:, h, vc * 128:(vc + 1) * 128], qe[:, h, :Lq], True, False, ["Sbf", "qe"], [ppk])
                        MM(o_, vtok[:Lq, b, h * 256 + vc * 128:h * 256 + (vc + 1) * 128], ATm[:Lq, h, :Lq], False, True,
                           ["vtok", "ATm"], [ppk])
                for i in range(2):
                    pp, ppk = po[i]
                    ACT(osq[:, i * 4:(i + 1) * 4, :Lq], pp[:, 0:4 * Lq].rearrange("p (h t) -> p h t", h=4), AF.Square,
                        [ppk], ["osq"])
                pn, pnk = ps()
                for h in range(4):
                    for vc in range(2):
                        MM(pn[:, h * Lq:(h + 1) * Lq], onesb[:, :], osq[:, h * 2 + vc, :Lq], vc == 0, vc == 1,
                           ["onesb", "osq"], [pnk])
                pn3 = pn[:, 0:4 * Lq].rearrange("p (h t) -> p h t", h=4)
                ACT(grs[:, :, :Lq], pn3, AF.Ln, [pnk], ["grs"], scale=1.0 / 256, bias=1e-6)
                ACT(grs[:, :, :Lq], grs[:, :, :Lq], AF.Exp, ["grs"], ["grs"], scale=-0.5)
                for h in range(4):
                    for vc in range(2):
                        i8 = h * 2 + vc
                        pp, ppk = po[i8 // 4]
                        o_ = pp[:, (i8 % 4) * Lq:(i8 % 4 + 1) * Lq]
                        STT("dve", sig[:, 0:Lq], o_, pv[:, PV_GN + vc:PV_GN + vc + 1], grs[:, h, :Lq], ALU.mult, ALU.mult,
                            [ppk, "pv", "grs"], ["sig"])
                        TT("dve", br[:, i8, tk], sig[:, 0:Lq], grT[:, i8, tk], ALU.mult, ["sig", "grT"], ["br"])
                for half in range(2):
                    pp, ppk = ps()
                    for hh in range(2):
                        h = half * 2 + hh
                        MM(pp[:, hh * 256:(hh + 1) * 256], kd[:Lq, h * 128:(h + 1) * 128], vtok[:Lq, b, h * 256:(h + 1) * 256],
                           True, True, ["kd", "vtok"], [ppk])
                    for hh in range(2):
                        h = half * 2 + hh
                        STT("dve", Sst[:, h, :], Sst[:, h, :], E1[:, h, Lq - 1:Lq], pp[:, hh * 256:(hh + 1) * 256],
                            ALU.mult, ALU.add, ["Sst", "E1", ppk], ["Sst"])
                if sample:
                    finals.append(DMA("sp", glas[l, b].rearrange("h d v -> d h v"), Sst[:], ["Sst"], (), "sso"))
                else:
                    CP("act", Sbf[:], Sst[:], ["Sst"], ["Sbf"])
                    if last and b == nblk - 1:
                        finals.append(DMA("sp", glap[l].rearrange("h d v -> d h v"), Sst[:], ["Sst"], (), "sso"))

            def branch_merge(bi):
                wb = w_branch[l, bi]
                for blk4 in range(4):
                    wv, wk = wget(wb, 0, 8, blk4 * 512, 512)
                    pls = []
                    for o in range(4):
                        p, pk = ps()
                        for kc in range(8):
                            MM(p[:, 0:NT], wv[:, kc, o * 128:(o + 1) * 128], br[:, kc, 0:NT], kc == 0, kc == 7, [wk, "br"], [pk])
                        pls.append((p, pk))
                    wg, wgk = wget(win, 0, 16, C_GATE + bi * D + blk4 * 512, 512)
                    for o in range(4):
                        oc = blk4 * 4 + o
                        p2, pk2 = ps()
                        for kc in range(16):
                            MM(p2[:, 0:NT], wg[:, kc, o * 128:(o + 1) * 128], hT[:, kc, 0:NT], kc == 0, kc == 15, [wgk, "hT"], [pk2])
                        ACT(sig[:, 0:NT], p2[:, 0:NT], AF.Sigmoid, [pk2], ["sig"])
                        p, pk = pls[o]
                        if bi == 0:
                            TT("dve", mg[:, oc, 0:NT], p[:, 0:NT], sig[:, 0:NT], ALU.mult, [pk, "sig"], ["mg"])
                        else:
                            TT("dve", tmpb[:, 0:NT], p[:, 0:NT], sig[:, 0:NT], ALU.mult, [pk, "sig"], ["tmpb"])
                            TT("pool", mg[:, oc, 0:NT], mg[:, oc, 0:NT], tmpb[:, 0:NT], ALU.add, ["mg", "tmpb"], ["mg"])

            if STAGE < 6:
                return
            branch_merge(0)
            if STAGE < 7:
                return

            enter_group(2)

            def ev_sq(oc, p, pk):
                CP("act" if oc % 2 else "dve", sqT[:, oc, 0:NT], p[:, 0:NT], [pk], ["sqT"])
            proj_F(win, C_SQ, 1024, 16, hT, NT, ev_sq, "hT")

            def ev_skv(b, c0, ncl, p, pk):
                CP("act", skv[:Lq, b, :], p[:Lq, 0:256], [pk], ["skv"])
            proj_T(win, C_SK, 256, 16, hT, Lq, nblk, ev_skv, "hT")

            for b in range(nblk):
                tk = slice(b * Lq, (b + 1) * Lq)
                has_prev = sample or not (first and b == 0)
                pslot = 0 if sample else b
                cslot = b + 1
                if sample:
                    DMA("sp", stg[:, 0, 0:128], csk[l, b], (), ["stg"], "stg")
                    DMA("sp", stg[:, 1, 0:128], csv[l, b], (), ["stg"], "stg")
                    finals.append(DMA("sp", swaks[l, b, 0:128 - TS, :], csk[l, b, TS:128, :], (), (), "d2d"))
                    finals.append(DMA("sp", swavs[l, b, 0:128 - TS, :], csv[l, b, TS:128, :], (), (), "d2d"))
                    for kv in range(2):
                        for dd in range(2):
                            CP("dve", kdup[:, kv, dd * 64:(dd + 1) * 64], stg[:, 0, kv * 64:(kv + 1) * 64], ["stg"], ["kdup"])
                            CP("pool", Vd[:, 0, kv, dd * 64:(dd + 1) * 64], stg[:, 1, kv * 64:(kv + 1) * 64], ["stg"], ["Vd"])
                    for kv in range(2):
                        TR(pstT[:, kv, :], kdup[:, kv, :], identb[:], ["kdup", "identb"], ["pst"])
                    CP("dve", KTd[:, :, 0:128], pstT[:, 0:2, :], ["pst"], ["KTd"])
                for kv in range(2):
                    for dd in range(2):
                        CP("dve", kdup[:Lq, kv, dd * 64:(dd + 1) * 64], skv[:Lq, b, kv * 64:(kv + 1) * 64], ["skv"], ["kdup"])
                        CP("pool", Vd[:Lq, cslot, kv, dd * 64:(dd + 1) * 64], skv[:Lq, b, 128 + kv * 64:128 + (kv + 1) * 64],
                           ["skv"], ["Vd"])
                for kv in range(2):
                    TR(pstT[:, kv, :Lq], kdup[:Lq, kv, :], identb[:Lq, :Lq], ["kdup", "identb"], ["pst"])
                CP("dve", KTd[:, :, cslot * 128:cslot * 128 + Lq], pstT[:, 0:2, :Lq], ["pst"], ["KTd"])
                if sample:
                    finals.append(DMA("sp", swaks[l, b, 128 - TS:128, :], skv[:Lq, b, 0:128], ["skv"], (), "skvo"))
                    finals.append(DMA("sp", swavs[l, b, 128 - TS:128, :], skv[:Lq, b, 128:256], ["skv"], (), "skvo"))
                elif last and b == nblk - 1:
                    finals.append(DMA("sp", swakp[l], skv[:Lq, b, 0:128], ["skv"], (), "skvo"))
                    finals.append(DMA("sp", swavp[l], skv[:Lq, b, 128:256], ["skv"], (), "skvo"))
                for kv in range(2):
                    for par in range(2):
                        hs = slice(par * 64, (par + 1) * 64)
                        hsel = slice(kv * 8 + par, kv * 8 + 8, 2)
                        qrhs = sqT[hs, kv * 4:kv * 4 + 4, tk]
                        pc, pck = ps()
                        MM(pc[:Lq, 0:4 * Lq].rearrange("p (h t) -> p h t", h=4), KTd[hs, kv, cslot * 128:cslot * 128 + Lq], qrhs,
                           True, True, ["KTd", "sqT"], [pck])
                        STT("dve", sbias[:Lq, 0:4 * Lq].rearrange("p (h t) -> p h t", h=4),
                            pc[:Lq, 0:4 * Lq].rearrange("p (h t) -> p h t", h=4), 0.125, biasC[:Lq, hsel, :Lq],
                            ALU.mult, ALU.add, [pck, "bias"], ["sbias"])
                        ACT(PTc[:Lq, :, :Lq], sbias[:Lq, 0:4 * Lq].rearrange("p (h t) -> p h t", h=4), AF.Exp, ["sbias"], ["PTc"])
                        if has_prev:
                            pp_, ppk = ps()
                            MM(pp_[:, 0:4 * Lq].rearrange("p (h t) -> p h t", h=4), KTd[hs, kv, pslot * 128:(pslot + 1) * 128], qrhs,
                               True, True, ["KTd", "sqT"], [ppk])
                            STT("dve", sbias[:, 0:4 * Lq].rearrange("p (h t) -> p h t", h=4),
                                pp_[:, 0:4 * Lq].rearrange("p (h t) -> p h t", h=4), 0.125, biasP[:, hsel, :Lq],
                                ALU.mult, ALU.add, [ppk, "bias"], ["sbias"])
                            ACT(PTp[:, :, :Lq], sbias[:, 0:4 * Lq].rearrange("p (h t) -> p h t", h=4), AF.Exp, ["sbias"], ["PTp"])
                        po_, pok = ps()
                        pd_, pdk = ps()
                        for j in range(4):
                            o_ = po_[:, j * Lq:(j + 1) * Lq]
                            if has_prev:
                                MM(o_, Vd[:, pslot, kv, :], PTp[:, j, :Lq], True, False, ["Vd", "PTp"], [pok])
                            MM(o_, Vd[:Lq, cslot, kv, :], PTc[:Lq, j, :Lq], not has_prev, True, ["Vd", "PTc"], [pok])
                        for j in range(4):
                            d_ = pd_[:, j * Lq:(j + 1) * Lq]
                            if has_prev:
                                MM(d_, onesb[:, :], PTp[:, j, :Lq], True, False, ["onesb", "PTp"], [pdk])
                            MM(d_, onesb[:Lq, :], PTc[:Lq, j, :Lq], not has_prev, True, ["onesb", "PTc"], [pdk])
                        for j in range(4):
                            hh = kv * 8 + par + 2 * j
                            TS_("dve", rden[:, j, :Lq], pd_[:, j * Lq:(j + 1) * Lq], esink[:, hh:hh + 1], None, ALU.add, None,
                                [pdk, "esink"], ["rden"])
                        S.add("dve", lambda e: e.reciprocal(out=rden[:, :, :Lq], in_=rden[:, :, :Lq]), ["rden"], ["rden"])
                        TT("dve", br[hs, kv * 4:kv * 4 + 4, tk], po_[hs, 0:4 * Lq].rearrange("p (h t) -> p h t", h=4),
                           rden[hs, :, :Lq], ALU.mult, [pok, "rden"], ["br"])
            if not sample:
                CP("dve", KTd[:, :, 0:128], KTd[:, :, 4 * 128:5 * 128], ["KTd"], ["KTd"])
                CP("pool", Vd[:, 0, :, :], Vd[:, 4, :, :], ["Vd"], ["Vd"])
            if STAGE < 8:
                return
            branch_merge(1)

            enter_group(3)

            def tail_load(src2, nch, dstbuf, s_, dkey):
                n2 = 2 * nch
                DMA("sp", tl[0:n2, :], src2.rearrange("r (c f) -> (r c) f", f=128), (), ["tl"], "tl")
                p, pk = ps()
                MM(p[:, 0:n2], tl[0:n2, :], identf[0:n2, 0:n2], True, True, ["tl", "identf"], [pk])
                CP("dve", dstbuf[:, 0:nch, s_, :], p[:, 0:n2].rearrange("p (r c) -> p c r", r=2), [pk], [dkey])

            def tail_store(srcbuf, nch, s_, skey, dst2):
                n2 = 2 * nch
                CP("dve", tls[:, 0:n2].rearrange("p (r c) -> p c r", r=2), srcbuf[:, 0:nch, s_, :], [skey], ["tls"])
                p, pk = ps()
                MM(p[0:n2, 0:128], tls[:, 0:n2], identf[:, :], True, True, ["tls", "identf"], [pk])
                CP("dve", tlo[0:n2, :], p[0:n2, 0:128], [pk], ["tlo"])
                finals.append(DMA("sp", dst2.rearrange("r (c f) -> (r c) f", f=128), tlo[0:n2, :], ["tlo"], (), "tlo"))

            def ev_cb(oc, p, pk):
                CP("act", cbT[:, oc, 0:NT], p[:, 0:NT], [pk], ["cbT"])
            proj_F(win, C_CB, 1024, 16, hT, NT, ev_cb, "hT")

            def ev_cc(oc, p, pk):
                CP("act", ccT[:, oc, 0:NT], p[:, 0:NT], [pk], ["ccT"])
            proj_F(win, C_CC, 1024, 16, hT, NT, ev_cc, "hT")

            up4 = upad[:, :, 0:nseg * (sl + 2)].rearrange("p c (s t) -> p c s t", s=nseg)
            if first:
                MS("dve", ctail[:], 0.0, ["ctail"])
            if sample:
                for s_ in range(nseg):
                    tail_load(sconv[l, s_], 8, ctail, s_, "ctail")
            CP("dve", up4[:, :, :, 0:2], ctail[:, :, 0:nseg, :], ["ctail"], ["upad"])

            def ev_ch(oc, p, pk):
                TT("dve", up4[:, oc, :, 2:sl + 2], p[:, 0:NT].rearrange("p (s t) -> p s t", s=nseg),
                   ccT[:, oc, 0:NT].rearrange("p (s t) -> p s t", s=nseg), ALU.mult, [pk, "ccT"], ["upad"])
                c3 = ctmp[:, 0:NT].rearrange("p (s t) -> p s t", s=nseg)
                TS_("pool", c3, up4[:, oc, :, 0:sl], pv[:, PV_CW + oc:PV_CW + oc + 1], None, ALU.mult, None, ["upad", "pv"], ["ctmp"])
                STT("dve", c3, up4[:, oc, :, 1:sl + 1], pv[:, PV_CW + 8 + oc:PV_CW + 8 + oc + 1], c3, ALU.mult, ALU.add,
                    ["upad", "pv", "ctmp"], ["ctmp"])
                STT("dve", c3, up4[:, oc, :, 2:sl + 2], pv[:, PV_CW + 16 + oc:PV_CW + 16 + oc + 1], c3, ALU.mult, ALU.add,
                    ["upad", "pv", "ctmp"], ["ctmp"])
                TT("pool", br[:, oc, 0:NT], ctmp[:, 0:NT], cbT[:, oc, 0:NT], ALU.mult, ["ctmp", "cbT"], ["br"])
            proj_F(win, C_CH, 1024, 16, hT, NT, ev_ch, "hT")
            CP("dve", ctail[:, :, 0:nseg, :], up4[:, :, :, sl:sl + 2], ["upad"], ["ctail"])
            if sample:
                for s_ in range(nseg):
                    tail_store(ctail, 8, s_, "ctail", convs[l, s_])
            elif last:
                tail_store(ctail, 8, 0, "ctail", convp[l])
            if STAGE < 9:
                return
            branch_merge(2)

            def ev_res(b, c0, ncl, p, pk):
                TT("dve", xres[:Lq, b, c0:c0 + ncl], xres[:Lq, b, c0:c0 + ncl], p[:Lq, 0:ncl], ALU.add, [pk, "xres%d" % b], ["xres%d" % b])
            proj_T(w_out[l], 0, D, 16, mg, Lq, nblk, ev_res, "mg")

            if STAGE < 10:
                return
            norm_to_hT(Lq, nblk, pv[:, PV_NX:PV_NX + 16])

            enter_group(4)

            def ev_qx(oc, p, pk):
                CP("act", qxT[:, oc, 0:NT], p[:, 0:NT], [pk], ["qxT"])
            proj_F(wx_q[l], 0, 512, 16, hT, NT, ev_qx, "hT")
            for b in range(nblk):
                tk = slice(b * Lq, (b + 1) * Lq)
                if sample:
                    DMA("sp", stgx[:], cmk[l, b].rearrange("(c p) n -> p c n", p=128), (), ["stgx"], "stgx")
                    for mc in range(2):
                        CP("dve", vtokx[:, mc, :], stgx[:, mc, :], ["stgx"], ["vtokx"])
                    for mc in range(2):
                        for h in range(4):
                            TR(pstT[:, mc * 4 + h, :], vtokx[:, mc, h * 128:(h + 1) * 128], identb[:], ["vtokx", "identb"], ["pst"])
                    for h in range(4):
                        for mc in range(2):
                            CP("dve", KxT[:, h, mc * 128:(mc + 1) * 128], pstT[:, mc * 4 + h, :], ["pst"], ["KxT"])
                    DMA("sp", stgx[:], cmv[l, b].rearrange("(c p) n -> p c n", p=128), (), ["stgx"], "stgx")
                    CP("pool", Vx[:], stgx[:], ["stgx"], ["Vx"])
                for h in range(4):
                    pp_, ppk = ps()
                    for mc in range(2):
                        MM(pp_[:, mc * Lq:(mc + 1) * Lq], KxT[:, h, mc * 128:(mc + 1) * 128], qxT[:, h, tk], True, True, ["KxT", "qxT"], [ppk])
                    ACT(PTx[:, :, :Lq], pp_[:, 0:2 * Lq].rearrange("p (c t) -> p c t", c=2), AF.Exp, [ppk], ["PTx"], scale=128 ** -0.5)
                    po_, pok = ps()
                    for mc in range(2):
                        MM(po_[:, 0:Lq], Vx[:, mc, h * 128:(h + 1) * 128], PTx[:, mc, :Lq], mc == 0, mc == 1, ["Vx", "PTx"], [pok])
                    for mc in range(2):
                        MM(po_[:, 128:128 + Lq], onesb[:, :], PTx[:, mc, :Lq], mc == 0, mc == 1, ["onesb", "PTx"], [pok])
                    S.add("dve", lambda e, po_=po_: e.reciprocal(out=rdenx[:, :Lq], in_=po_[:, 128:128 + Lq]), [pok], ["rdenx"])
                    TT("dve", oxT[:, h, tk], po_[:, 0:Lq], rdenx[:, :Lq], ALU.mult, [pok, "rdenx"], ["oxT"])
            proj_T(wx_o[l], 0, D, 4, oxT, Lq, nblk, ev_res, "oxT")

            if STAGE < 11:
                return
            norm_to_hT(Lq, nblk, pv[:, PV_NF:PV_NF + 16])
            enter_group(5)
            gt4 = gtail[:, :, 0:nseg, :]
            if first:
                MS("dve", gtail[:], 0.0, ["gtail"])
            if sample:
                for s_ in range(nseg):
                    tail_load(sffn[l, s_], 43, gtail, s_, "gtail")
            gp3 = gpad[:, 0:nseg * (sl + 2)].rearrange("p (s t) -> p s t", s=nseg)
            wup = ffn_up[l]
            wdn = ffn_down[l]
            for piece in range(3):
                c_lo = piece * 16
                c_hi = min(43, c_lo + 16)
                ncp = c_hi - c_lo
                cc = c_lo
                while cc < c_hi:
                    n = min(4, c_hi - cc)
                    wu, wuk = wget(wup, 0, 16, cc * 128, n * 128)
                    pus = []
                    for o in range(n):
                        p, pk = ps()
                        for kc in range(16):
                            MM(p[:, 0:NT], wu[:, kc, o * 128:(o + 1) * 128], hT[:, kc, 0:NT], kc == 0, kc == 15, [wuk, "hT"], [pk])
                        pus.append((p, pk))
                    wg_, wgk = wget(wup, 0, 16, DFF + cc * 128, n * 128)
                    for o in range(n):
                        c = cc + o
                        p2, pk2 = ps()
                        for kc in range(16):
                            MM(p2[:, 0:NT], wg_[:, kc, o * 128:(o + 1) * 128], hT[:, kc, 0:NT], kc == 0, kc == 15, [wgk, "hT"], [pk2])
                        CP("pool", gp3[:, :, 0:2], gt4[:, c, :, :], ["gtail"], ["gpad"])
                        ACT(gp3[:, :, 2:sl + 2], p2[:, 0:NT].rearrange("p (s t) -> p s t", s=nseg), AF.Copy, [pk2], ["gpad"])
                        CP("pool", gt4[:, c, :, :], gp3[:, :, sl:sl + 2], ["gpad"], ["gtail"])
                        c3 = ctmp5[:, 0:NT].rearrange("p (s t) -> p s t", s=nseg)
                        TS_("pool", c3, gp3[:, :, 0:sl], pv[:, PV_FW + c:PV_FW + c + 1], None, ALU.mult, None, ["gpad", "pv"], ["ctmp5"])
                        STT("dve", c3, gp3[:, :, 1:sl + 1], pv[:, PV_FW + 43 + c:PV_FW + 43 + c + 1], c3, ALU.mult, ALU.add,
                            ["gpad", "pv", "ctmp5"], ["ctmp5"])
                        STT("dve", c3, gp3[:, :, 2:sl + 2], pv[:, PV_FW + 86 + c:PV_FW + 86 + c + 1], c3, ALU.mult, ALU.add,
                            ["gpad", "pv", "ctmp5"], ["ctmp5"])
                        ACT(sig[:, 0:NT], ctmp5[:, 0:NT], AF.Silu, ["ctmp5", "pv"], ["sig"], bias=pv[:, PV_FB + c:PV_FB + c + 1])
                        p, pk = pus[o]
                        TT("dve", mg[:, c - c_lo, 0:NT], p[:, 0:NT], sig[:, 0:NT], ALU.mult, [pk, "sig"], ["mg"])
                    cc += n
                proj_T(wdn, 0, D, ncp, mg, Lq, nblk, ev_res, "mg", r0=c_lo)
            if sample:
                for s_ in range(nseg):
                    tail_store(gtail, 43, s_, "gtail", ffns[l, s_])
            elif last:
                tail_store(gtail, 43, 0, "gtail", ffnp[l])

            if STAGE < 12:
                return
            if l == DEPTH - 1:
                dst = ys if sample else yp

                def rows(b):
                    if sample:
                        return dst[b * Lq:(b + 1) * Lq, :]
                    return dst[xrow(b), :]
                final_norm_store(Lq, nblk, rows)
            else:
                for b in range(nblk):
                    DMA("sp", xd[xrow(b), :], xres[:Lq, b, :], ["xres%d" % b], [("xd", kind, ti, b)], "xst%d" % b)

        def mem_kv(l):
            for mb in range(2):
                DMA("sp", xres[:, mb, :], memp[mb * 128:(mb + 1) * 128, :], (), ["xres%d" % mb], "xl%d" % mb)
                CP("dve", h0[:, :], xres[:, mb, :], ["xres%d" % mb], ["h0"])
                for half in range(2):
                    for j in range(8):
                        c = half * 8 + j
                        TR(pstT[:, j, :], h0[:, c * 128:(c + 1) * 128], identb[:], ["h0", "identb"], ["pst"])
                    CP("dve", hT[:, half * 8:(half + 1) * 8, mb * 128:(mb + 1) * 128], pstT[:, :, :], ["pst"], ["hT"])

            enter_group(6)

            def ev_kx(oc, p, pk):
                CP("act", KxT[:, oc, :], p[:, 0:256], [pk], ["KxT"])
            proj_F(wx_k[l], 0, 512, 16, hT, 256, ev_kx, "hT")

            def ev_kt(b, c0, ncl, p, pk):
                CP("act", ystage[:, 0:512], p[:, :], [pk], ["ystage"])
                finals.append(DMA("sp", memkp[l, b * 128:(b + 1) * 128, :], ystage[:, 0:512], ["ystage"], (), "yst"))
            proj_T(wx_k[l], 0, 512, 16, hT, 128, 2, ev_kt, "hT")

            def ev_vt(b, c0, ncl, p, pk):
                CP("act", ystage[:, 0:512], p[:, :], [pk], ["ystage"])
                CP("dve", Vx[:, b, :], p[:, :], [pk], ["Vx"])
                finals.append(DMA("sp", memvp[l, b * 128:(b + 1) * 128, :], ystage[:, 0:512], ["ystage"], (), "yst"))
            proj_T(wx_v[l], 0, 512, 16, hT, 128, 2, ev_vt, "hT")

        def emit_all():
            ps_i[0] = 0
            setup()
            if STAGE < 1:
                return
            for l in range(DEPTH):
                layer_setup(l)
                if STAGE < 2:
                    return
                mem_kv(l)
                if STAGE < 3:
                    return
                for ti in range(NPT):
                    do_tile(l, "p", ti)
                if STAGE < 20:
                    return
                do_tile(l, "s", 0)

        S.planning = True
        emit_all()
        S.planning = False
        del finals[:]
        emit_all()
        assert WS.pos == len(WS.plan), (WS.pos, len(WS.plan))
        S.finish(finals)
    return nc


_CACHE = {}


def run(inputs, SEQ, DEPTH, NB, DEC_BATCH, TS=8):
    NSQ = DEC_BATCH // 8
    key = (SEQ, DEPTH, NSQ, TS, STAGE)
    if key not in _CACHE:
        _CACHE[key] = build(SEQ, DEPTH, NSQ, TS)
    nc = _CACHE[key]
    f = lambda a: np.ascontiguousarray(np.asarray(a, dtype=np.float32))
    consts = host_consts()
    shared = {}
    for k in ("norm_mix", "w_in", "gla_gate_up", "gla_gate_b", "gla_norm", "swa_sinks", "rel_bias", "conv_w",
              "w_branch", "w_out", "norm_x", "wx_q", "wx_k", "wx_v", "wx_o", "norm_ffn", "ffn_up", "ffn_conv_w",
              "ffn_conv_b", "ffn_down", "norm_final"):
        shared[k] = f(inputs[k])
    shared.update(consts)
    xpr = f(inputs["x_prompt"])
    xsm = f(inputs["x_sample"])
    in_maps = []
    for c in range(8):
        pb = c % NB
        ss = slice(c * NSQ, (c + 1) * NSQ)
        m = dict(shared)
        m["xp"] = xpr[pb]
        m["xs"] = f(xsm[ss].reshape(NSQ * TS, D))
        m["sgla"] = f(inputs["state_gla"][:, ss])
        m["csk"] = f(np.asarray(inputs["cache_swa_k"])[:, ss].reshape(DEPTH, NSQ, 128, 128))
        m["csv"] = f(np.asarray(inputs["cache_swa_v"])[:, ss].reshape(DEPTH, NSQ, 128, 128))
        m["sconv"] = f(inputs["state_conv"][:, ss])
        m["sffn"] = f(inputs["state_ffn"][:, ss])
        m["cmk"] = f(np.asarray(inputs["cache_mem_k"])[:, ss].reshape(DEPTH, NSQ, 256, 512))
        m["cmv"] = f(np.asarray(inputs["cache_mem_v"])[:, ss].reshape(DEPTH, NSQ, 256, 512))
        m["memp"] = f(inputs["mem_prompt"][pb])
        in_maps.append(m)
    res = run_bass_kernel_spmd(nc, in_maps, core_ids=list(range(8)))
    R = res.results
    cat = lambda k, ax: np.concatenate([R[c][k] for c in range(8)], axis=ax)
    stk = lambda k: np.stack([R[c][k] for c in range(NB)], axis=1)
    y_prompt = np.stack([R[c]["yp"] for c in range(NB)], axis=0)
    y_sample = cat("ys", 0).reshape(DEC_BATCH, TS, D)
    outs = (
        y_prompt, y_sample,
        stk("glap"), cat("glas", 1),
        stk("swakp").reshape(DEPTH, NB, 128, 2, 64), stk("swavp").reshape(DEPTH, NB, 128, 2, 64),
        cat("swaks", 1).reshape(DEPTH, DEC_BATCH, 128, 2, 64), cat("swavs", 1).reshape(DEPTH, DEC_BATCH, 128, 2, 64),
        stk("convp"), cat("convs", 1), stk("ffnp"), cat("ffns", 1),
        stk("memkp").reshape(DEPTH, NB, 256, 4, 128), stk("memvp").reshape(DEPTH, NB, 256, 4, 128),
    )
    return tuple(np.ascontiguousarray(o, dtype=np.float32) for o in outs)


def kernel(**inputs):
    return run(inputs, 4096, 4, 2, 32)
```

```python
import math
import numpy as np
from contextlib import ExitStack
import concourse.bass as bass
import concourse.mybir as mybir
from concourse.bass_utils import run_bass_kernel_spmd

F32 = mybir.dt.float32
BF16 = mybir.dt.bfloat16
AF = mybir.ActivationFunctionType
ALU = mybir.AluOpType

D = 2048
MIX = 1024
DFF = 5504
NIN = 13584
C_GQ, C_GK, C_GV, C_GR, C_GLR, C_SQ, C_SK, C_SV, C_CB, C_CC, C_CH, C_GATE = (
    0, 512, 1024, 2048, 3072, 3088, 4112, 4240, 4368, 5392, 6416, 7440)
NEG = -30000.0
ENGS = ("pe", "act", "dve", "pool", "sp")


class Op:
    __slots__ = ("eng", "fn", "seq", "needed", "sig", "is_dma", "dsem", "dval", "deps")

    def __init__(self, eng, fn):
        self.eng = eng
        self.fn = fn
        self.deps = []
        self.needed = False
        self.sig = None
        self.is_dma = False
        self.dsem = None
        self.dval = 0


class Sched:
    def __init__(self, nc, stack):
        self.nc = nc
        self.stack = stack
        self.planning = False
        self.reset()
        self.sem = {e: stack.enter_context(nc.semaphore("s_" + e)) for e in ENGS}
        self.dma_sems = {}

    def reset(self):
        self.ops = {e: [] for e in ENGS}
        self.last_w = {}
        self.readers = {}
        self.waited = {e: {} for e in ENGS}
        self.dma_cnt = {}
        self.fake_w = set()

    def _dep(self, op, p):
        if p is None or p is op:
            return
        w = self.waited[op.eng]
        if p.is_dma:
            k = ("d", id(p.dsem))
            if w.get(k, -1) >= p.dval:
                return
            w[k] = p.dval
            op.deps.append(p)
            return
        if p.eng == op.eng and op.eng == "pe" and not op.is_dma:
            return
        if w.get(p.eng, -1) >= p.seq:
            return
        w[p.eng] = p.seq
        p.needed = True
        op.deps.append(p)

    def add(self, eng, fn, reads=(), writes=(), dma_stream=None):
        if self.planning:
            return None
        op = Op(eng, fn)
        op.seq = len(self.ops[eng])
        if dma_stream is not None:
            op.is_dma = True
            if dma_stream not in self.dma_sems:
                self.dma_sems[dma_stream] = self.stack.enter_context(self.nc.semaphore("d_" + str(dma_stream)))
            self.dma_cnt[dma_stream] = self.dma_cnt.get(dma_stream, 0) + 16
            op.dsem = self.dma_sems[dma_stream]
            op.dval = self.dma_cnt[dma_stream]
        psr = [k for k in reads if isinstance(k, str) and k.startswith("ps")]
        for k in reads:
            lw = self.last_w.get(k)
            if lw is not None and k in psr and (id(lw), k) in self.fake_w and lw.eng == eng:
                continue
            self._dep(op, lw)
        for k in writes:
            self._dep(op, self.last_w.get(k))
            rd = self.readers.get(k)
            if rd:
                for r in rd.values():
                    self._dep(op, r)
        rk = eng if not op.is_dma else ("dma", dma_stream)
        for k in reads:
            self.readers.setdefault(k, {})[rk] = op
        for k in writes:
            self.last_w[k] = op
            self.readers[k] = {}
        for k in psr:
            if k not in writes:
                self.last_w[k] = op
                self.fake_w.add((id(op), k))
        self.ops[eng].append(op)
        return op

    def finish(self, final_waits=()):
        nc = self.nc
        for e in ENGS:
            c = 0
            for op in self.ops[e]:
                if op.needed and not op.is_dma:
                    c += 1
                    op.sig = c
        fin = Op("sp", None)
        seen = set()
        for p in final_waits:
            if p is not None and id(p) not in seen:
                seen.add(id(p))
                fin.deps.append(p)
        sems = self.sem
        ops = self.ops

        def emit_engine(e, eng):
            for op in ops[e] + ([fin] if e == "sp" else []):
                deps = op.deps
                attach = None
                if deps and op.fn is not None and not op.is_dma:
                    attach = deps[-1]
                    deps = deps[:-1]
                for p in deps:
                    if p.is_dma:
                        eng.wait_ge(p.dsem, p.dval)
                    else:
                        eng.wait_ge(sems[p.eng], p.sig)
                if op.fn is None:
                    continue
                ins = op.fn(eng)
                if attach is not None:
                    if attach.is_dma:
                        ins._wait_ge(attach.dsem, attach.dval)
                    else:
                        ins._wait_ge(sems[attach.eng], attach.sig)
                if op.is_dma:
                    ins.then_inc(op.dsem, 16)
                elif op.needed:
                    ins.then_inc(sems[e], 1)

        with nc.Block() as block:
            @block.tensor
            def _(eng):
                emit_engine("pe", eng)

            @block.scalar
            def _(eng):
                emit_engine("act", eng)

            @block.vector
            def _(eng):
                emit_engine("dve", eng)

            @block.gpsimd
            def _(eng):
                emit_engine("pool", eng)

            @block.sync
            def _(eng):
                emit_engine("sp", eng)


def t5_bucket_np(dist):
    n = np.maximum(dist, 0)
    nf = np.maximum(n, 1).astype(np.float32)
    large = 16 + (np.log(nf / np.float32(16)) / np.float32(math.log(128 / 16)) * np.float32(16)).astype(np.int32)
    large = np.minimum(large, 31)
    return np.where(n < 16, n, large)


def host_consts():
    c = {}
    c["c_ident"] = np.eye(128, dtype=np.float32)
    s = np.arange(128)[:, None]
    t = np.arange(128)[None, :]
    c["c_triS"] = np.where(s <= t, -1.0 / 16, 0.0).astype(np.float32)
    c["c_triU"] = np.where(s > t, -1.0 / 16, 0.0).astype(np.float32)
    c["c_mask"] = (s <= t).astype(np.float32)
    c["c_J"] = np.eye(128, dtype=np.float32)[::-1].copy()
    n = np.arange(256)
    u = n - 127
    oh = np.zeros((32, 512), np.float32)
    ng = np.zeros((16, 512), np.float32)
    dcur = u
    vcur = (dcur >= 0) & (dcur <= 128)
    bc = t5_bucket_np(np.clip(dcur, 0, 128))
    dprev = 128 + u
    vprev = (dprev >= 0) & (dprev <= 128)
    bp = t5_bucket_np(np.clip(dprev, 0, 128))
    for i in range(256):
        if vcur[i]:
            oh[bc[i], i] = 1.0
        else:
            ng[:, i] = NEG
        if vprev[i]:
            oh[bp[i], 256 + i] = 1.0
        else:
            ng[:, 256 + i] = NEG
    c["c_oh"] = oh
    c["c_ng"] = ng
    return c


import os
STAGE = int(os.environ.get("KSTAGE", "99"))


def build(SEQ, DEPTH, NSQ=4, TS=8):
    nc = bass.Bass("TRN2", target_bir_lowering=False)
    NPT = SEQ // 512
    NST = NSQ * TS

    def din(name, shape):
        return nc.dram_tensor(name, list(shape), F32, kind="ExternalInput").ap()

    def dout(name, shape):
        return nc.dram_tensor(name, list(shape), F32, kind="ExternalOutput").ap()

    xp = din("xp", [SEQ, D])
    xs = din("xs", [NST, D])
    sgla = din("sgla", [DEPTH, NSQ, 4, 128, 256])
    csk = din("csk", [DEPTH, NSQ, 128, 128])
    csv = din("csv", [DEPTH, NSQ, 128, 128])
    sconv = din("sconv", [DEPTH, NSQ, 2, MIX])
    sffn = din("sffn", [DEPTH, NSQ, 2, DFF])
    cmk = din("cmk", [DEPTH, NSQ, 256, 512])
    cmv = din("cmv", [DEPTH, NSQ, 256, 512])
    memp = din("memp", [256, D])
    norm_mix = din("norm_mix", [DEPTH, D])
    w_in = din("w_in", [DEPTH, D, NIN])
    gate_up = din("gla_gate_up", [DEPTH, 16, 512])
    gate_b = din("gla_gate_b", [DEPTH, 512])
    gla_norm = din("gla_norm", [DEPTH, 256])
    swa_sinks = din("swa_sinks", [DEPTH, 16])
    rel_bias = din("rel_bias", [32, 16])
    conv_w = din("conv_w", [DEPTH, 3, MIX])
    w_branch = din("w_branch", [DEPTH, 3, MIX, D])
    w_out = din("w_out", [DEPTH, D, D])
    norm_x = din("norm_x", [DEPTH, D])
    wx_q = din("wx_q", [DEPTH, D, 512])
    wx_k = din("wx_k", [DEPTH, D, 512])
    wx_v = din("wx_v", [DEPTH, D, 512])
    wx_o = din("wx_o", [DEPTH, 512, D])
    norm_ffn = din("norm_ffn", [DEPTH, D])
    ffn_up = din("ffn_up", [DEPTH, D, 2 * DFF])
    ffn_cw = din("ffn_conv_w", [DEPTH, 3, DFF])
    ffn_cb = din("ffn_conv_b", [DEPTH, DFF])
    ffn_down = din("ffn_down", [DEPTH, DFF, D])
    norm_final = din("norm_final", [D])
    c_ident = din("c_ident", [128, 128])
    c_triS = din("c_triS", [128, 128])
    c_triU = din("c_triU", [128, 128])
    c_mask = din("c_mask", [128, 128])
    c_J = din("c_J", [128, 128])
    c_oh = din("c_oh", [32, 512])
    c_ng = din("c_ng", [16, 512])

    yp = dout("yp", [SEQ, D])
    ys = dout("ys", [NST, D])
    glap = dout("glap", [DEPTH, 4, 128, 256])
    glas = dout("glas", [DEPTH, NSQ, 4, 128, 256])
    swakp = dout("swakp", [DEPTH, 128, 128])
    swavp = dout("swavp", [DEPTH, 128, 128])
    swaks = dout("swaks", [DEPTH, NSQ, 128, 128])
    swavs = dout("swavs", [DEPTH, NSQ, 128, 128])
    convp = dout("convp", [DEPTH, 2, MIX])
    convs = dout("convs", [DEPTH, NSQ, 2, MIX])
    ffnp = dout("ffnp", [DEPTH, 2, DFF])
    ffns = dout("ffns", [DEPTH, NSQ, 2, DFF])
    memkp = dout("memkp", [DEPTH, 256, 512])
    memvp = dout("memvp", [DEPTH, 256, 512])

    xd = nc.dram_tensor("xd", [SEQ + NST, D], F32, kind="Internal").ap()
    Rd_t = nc.dram_tensor("Rd", [16, 512], F32, kind="Internal")
    Rd = Rd_t.ap()

    finals = []
    with ExitStack() as st:
        S = Sched(nc, st)

        def sb(name, shape, dt=F32):
            return st.enter_context(nc.sbuf_tensor(name, list(shape), dt))

        identf = sb("identf", [128, 128])
        identb = sb("identb", [128, 128], BF16)
        onesb = sb("onesb", [128, 128], BF16)
        onesf = sb("onesf", [1, 128])
        triS = sb("triS", [128, 128])
        triU = sb("triU", [128, 128])
        mask4 = sb("mask4", [128, 4, 128])
        Jt = sb("Jt", [128, 128])
        biasC = sb("biasC", [128, 16, 128])
        biasP = sb("biasP", [128, 16, 128])
        pv = sb("pv", [128, 256])
        gfin = sb("gfin", [128, 16])
        esink = sb("esink", [128, 16])
        gup = sb("gup", [17, 512])
        xres = sb("xres", [128, 4, D])
        hT = sb("hT", [128, 16, 512], BF16)
        h0 = sb("h0", [128, D], BF16)
        nstat = sb("nstat", [128, 8])
        wsl = [sb("wsl%d" % i, [128, 16, 512], BF16) for i in range(2)]
        ABF = sb("ABF", [128, 17408], BF16)
        AFF = sb("AFF", [128, 4736], F32)

        def _view(t, np_, off, shape):
            n = 1
            for d in shape[1:]:
                n *= d
            v = t[0:np_, off:off + n]
            if len(shape) == 3:
                v = v.rearrange("p (a b) -> p a b", a=shape[1])
            elif len(shape) == 4:
                v = v.rearrange("p (a b c) -> p a b c", a=shape[1], b=shape[2])
            return v

        def vb(off, shape):
            return _view(ABF, shape[0], off, shape)

        def vf(off, shape):
            return _view(AFF, shape[0], off, shape)

        qT = vb(0, [128, 4, 512]); kT = vb(2048, [128, 4, 512]); ktok = vb(4096, [128, 4, 512])
        vtok = vb(6144, [128, 4, 1024]); grT = vb(10240, [128, 8, 512])
        qe = vb(14336, [128, 4, 128]); ke = vb(14848, [128, 4, 128]); kd = vb(15360, [128, 512])
        ATm = vb(15872, [128, 4, 128]); osq = vb(16384, [128, 8, 128])
        glr = vf(0, [17, 512]); l1 = vf(512, [128, 512]); E1 = vf(1024, [128, 4, 128]); E2 = vf(1536, [128, 4, 128])
        E3 = vf(2048, [128, 512]); grs = vf(2560, [128, 4, 128])
        sqT = vb(0, [128, 8, 512]); kdup = vb(4096, [128, 2, 128]); PTp = vb(4352, [128, 4, 128]); PTc = vb(4864, [128, 4, 128])
        skv = vf(0, [128, 4, 256]); stg = vf(1024, [128, 2, 512]); sbias = vf(2048, [128, 512]); rden = vf(2560, [128, 4, 128])
        cbT = vb(0, [128, 8, 512]); ccT = vb(4096, [128, 8, 512])
        upad = vf(0, [128, 8, 528]); ctmp = vf(4224, [128, 512])
        qxT = vb(0, [128, 4, 512]); oxT = vb(2048, [128, 4, 512]); PTx = vb(4096, [128, 2, 128]); vtokx = vb(4352, [128, 2, 512])
        stgx = vf(0, [128, 2, 512]); rdenx = vf(1024, [128, 128])
        gpad = vf(0, [128, 528]); ctmp5 = vf(528, [128, 512])
        ystage = vf(0, [128, D])
        GK = {
            1: ["qT", "kT", "ktok", "vtok", "grT", "qe", "ke", "kd", "ATm", "osq", "glr", "l1", "E1", "E2", "E3", "grs"],
            2: ["sqT", "kdup", "PTp", "PTc", "skv", "stg", "sbias", "rden"],
            3: ["cbT", "ccT", "upad", "ctmp"],
            4: ["qxT", "oxT", "PTx", "vtokx", "stgx", "rdenx"],
            5: ["gpad", "ctmp5"],
            6: ["ystage"],
        }

        def enter_group(g):
            if S.planning:
                return
            merged = {}
            for og, keys in GK.items():
                if og == g:
                    continue
                for ok in keys:
                    for rk, op in S.readers.get(ok, {}).items():
                        merged[(ok, rk)] = op
                    lw = S.last_w.get(ok)
                    if lw is not None:
                        merged[(ok, "w")] = lw
            for nk in GK[g]:
                d = dict(S.readers.get(nk, {}))
                d.update(merged)
                S.readers[nk] = d

        Sst = sb("Sst", [128, 4, 256])
        Sbf = sb("Sbf", [128, 4, 256], BF16)
        KTd = sb("KTd", [128, 2, 5 * 128], BF16)
        Vd = sb("Vd", [128, 5, 2, 128], BF16)
        br = sb("br", [128, 8, 512], BF16)
        mg = sb("mg", [128, 16, 512], BF16)
        sig = sb("sig", [128, 512])
        tmpb = sb("tmpb", [128, 512], BF16)
        KxT = sb("KxT", [128, 4, 256], BF16)
        Vx = sb("Vx", [128, 2, 512], BF16)
        gtail = sb("gtail", [128, 43, 4, 2])
        ctail = sb("ctail", [128, 8, 4, 2])
        tl = sb("tl", [128, 128])
        tls = sb("tls", [128, 128])
        tlo = sb("tlo", [128, 128])

        psb = [st.enter_context(nc.psum_tensor("ps%d" % i, [128, 512], F32)) for i in range(7)]
        pstT = st.enter_context(nc.psum_tensor("pst", [128, 8, 128], BF16))
        ps_i = [0]

        def ps():
            i = ps_i[0] % 7
            ps_i[0] += 1
            return psb[i], "ps%d" % i

        def MM(out, lhsT, rhs, start, stop, r, w):
            return S.add("pe", lambda e: e.matmul(out, lhsT=lhsT, rhs=rhs, start=start, stop=stop), r, w)

        def TR(out, in_, idt, r, w):
            return S.add("pe", lambda e: e.transpose(out=out, in_=in_, identity=idt), r, w)

        def ACT(out, in_, func, r, w, scale=None, bias=None, accum=None):
            kw = {}
            if scale is not None:
                kw["scale"] = scale
            if bias is not None:
                kw["bias"] = bias
            if accum is not None:
                kw["accum_out"] = accum
            return S.add("act", lambda e: e.activation(out=out, in_=in_, func=func, **kw), r, w)

        def TS_(eng, out, in0, s1, s2, op0, op1, r, w):
            if op1 is None:
                return S.add(eng, lambda e: e.tensor_scalar(out=out, in0=in0, scalar1=s1, scalar2=None, op0=op0), r, w)
            return S.add(eng, lambda e: e.tensor_scalar(out=out, in0=in0, scalar1=s1, scalar2=s2, op0=op0, op1=op1), r, w)

        def TT(eng, out, in0, in1, op, r, w):
            return S.add(eng, lambda e: e.tensor_tensor(out=out, in0=in0, in1=in1, op=op), r, w)

        def STT(eng, out, in0, sc, in1, op0, op1, r, w):
            return S.add(eng, lambda e: e.scalar_tensor_tensor(out=out, in0=in0, scalar=sc, in1=in1, op0=op0, op1=op1), r, w)

        def CP(eng, out, in_, r, w):
            if eng == "act":
                return S.add("act", lambda e: e.copy(out=out, in_=in_), r, w)
            return S.add(eng, lambda e: e.tensor_copy(out=out, in_=in_), r, w)

        def MS(eng, out, val, w):
            return S.add(eng, lambda e: e.memset(out, val), (), w)

        def DMA(q, out, in_, r, w, stream, slow=False):
            if slow:
                return S.add(q, lambda e: e.dma_start(out=out, in_=in_, allow_slow_non_contiguous=True), r, w, dma_stream=stream)
            return S.add(q, lambda e: e.dma_start(out=out, in_=in_), r, w, dma_stream=stream)

        class WS:
            plan = []
            pos = 0
            issued = 0

        def wget(src2d, r0, nkc, c0, ncols):
            spec = (src2d, r0, nkc, c0, ncols)
            if S.planning:
                WS.plan.append(spec)
                return wsl[0][:, 0:nkc, 0:ncols], "w0"
            i = WS.pos
            WS.pos += 1
            while WS.issued < min(len(WS.plan), i + 2):
                j = WS.issued
                s2, rr, nk, cc, ncl = WS.plan[j]
                slot = j % 2
                src = s2[rr * 128:(rr + nk) * 128, cc:cc + ncl].rearrange("(c p) n -> p c n", p=128)
                DMA("pool", wsl[slot][:, 0:nk, 0:ncl], src, (), ["w%d" % slot], "w%d" % slot)
                WS.issued += 1
            return wsl[i % 2][:, 0:nkc, 0:ncols], "w%d" % (i % 2)

        def setup():
            DMA("sp", identf[:], c_ident, (), ["identf"], "c0")
            DMA("sp", triS[:], c_triS, (), ["triS"], "c1")
            DMA("sp", triU[:], c_triU, (), ["triU"], "c2")
            DMA("sp", Jt[:], c_J, (), ["Jt"], "c3")
            for h in range(4):
                DMA("sp", mask4[:, h, :], c_mask, (), ["mask4"], "c4")
            CP("dve", identb[:], identf[:], ["identf"], ["identb"])
            MS("dve", onesb[:], 1.0, ["onesb"])
            MS("dve", onesf[:], 1.0, ["onesf"])
            vec_rows([(0, norm_final.rearrange("(c f) -> c f", f=128))], gfin, "gfin")
            DMA("sp", stg[0:32, 0, :], c_oh, (), ["stg"], "stg")
            DMA("sp", sig[0:32, 0:16], rel_bias, (), ["sig"], "c6")
            DMA("sp", l1[0:16, :], c_ng, (), ["l1"], "c7")
            p, pk = ps()
            MM(p[0:16, :], sig[0:32, 0:16], stg[0:32, 0, :], True, True, ["sig", "stg"], [pk])
            TT("dve", E3[0:16, :], p[0:16, :], l1[0:16, :], ALU.add, [pk, "l1"], ["E3"])
            DMA("sp", Rd, E3[0:16, :], ["E3"], ["Rd"], "rd")
            for which, dst in ((0, biasC), (1, biasP)):
                src = bass.AP(Rd_t, which * 256, [[1, 128], [512, 16], [1, 128]])
                DMA("sp", xres[:, 0, 0:2048].rearrange("p (h i) -> p h i", h=16), src, ["Rd"], ["xres0"], "xl0", slow=True)
                for hh in range(4):
                    p, pk = ps()
                    MM(p[:], Jt[:], xres[:, 0, hh * 512:(hh + 1) * 512], True, True, ["Jt", "xres0"], [pk])
                    CP("dve", dst[:, hh * 4:(hh + 1) * 4, :], p[:].rearrange("p (h i) -> p h i", h=4), [pk], ["bias"])

        PV_NM, PV_NX, PV_NF, PV_GN, PV_CW, PV_FW, PV_FB = 0, 16, 32, 48, 50, 74, 203

        def vec_rows(items, dst, dkey):
            tot = max(r0 + ap_.shape[0] for r0, ap_ in items)
            for base in range(0, tot, 128):
                n = min(128, tot - base)
                for r0, ap_ in items:
                    nr = ap_.shape[0]
                    lo = max(r0, base)
                    hi = min(r0 + nr, base + n)
                    if lo < hi:
                        DMA("sp", tl[lo - base:hi - base, :], ap_[lo - r0:hi - r0, :], (), ["tl"], "tl")
                p, pk = ps()
                MM(p[:, 0:n], tl[0:n, :], identf[0:n, 0:n], True, True, ["tl", "identf"], [pk])
                CP("dve", dst[:, base:base + n], p[:, 0:n], [pk], [dkey])

        def layer_setup(l):
            q = "sp"
            r2 = lambda v: v.rearrange("(c f) -> c f", f=128)
            items = [(PV_NM, r2(norm_mix[l])), (PV_NX, r2(norm_x[l])), (PV_NF, r2(norm_ffn[l])), (PV_GN, r2(gla_norm[l])),
                     (PV_CW, conv_w[l].rearrange("j (c f) -> (j c) f", f=128)),
                     (PV_FW, ffn_cw[l].rearrange("j (c f) -> (j c) f", f=128)), (PV_FB, r2(ffn_cb[l]))]
            vec_rows(items, pv, "pv")
            DMA(q, tl[0:1, 0:16], swa_sinks[l:l + 1, :], (), ["tl"], "tl")
            p, pk = ps()
            MM(p[:, 0:16], onesf[0:1, :], tl[0:1, 0:16], True, True, ["onesf", "tl"], [pk])
            ACT(esink[:], p[:, 0:16], AF.Exp, [pk], ["esink"])
            DMA(q, gup[0:16, :], gate_up[l], (), ["gup"], "gup")
            DMA(q, gup[16:17, :], gate_b[l:l + 1, :], (), ["gup"], "gup")

        def norm_to_hT(Lq, nblk, gcol, final_out=None):
            for b in range(nblk):
                xk = "xres%d" % b
                MS("dve", nstat[:Lq, 0:1], 0.0, ["nstat"])
                ACT(h0[:Lq, :], xres[:Lq, b, :], AF.Square, [xk], ["h0", "nstat"], accum=nstat[:Lq, 0:1])
                ACT(nstat[:Lq, 1:2], nstat[:Lq, 0:1], AF.Ln, ["nstat"], ["nstat"], scale=1.0 / D, bias=1e-6)
                ACT(nstat[:Lq, 2:3], nstat[:Lq, 1:2], AF.Exp, ["nstat"], ["nstat"], scale=-0.5)
                TS_("dve", h0[:Lq, :], xres[:Lq, b, :], nstat[:Lq, 2:3], None, ALU.mult, None, [xk, "nstat"], ["h0"])
                for half in range(2):
                    for j in range(8):
                        c = half * 8 + j
                        TR(pstT[:, j, :Lq], h0[:Lq, c * 128:(c + 1) * 128], identb[:Lq, :Lq], ["h0", "identb"], ["pst"])
                    for j in range(8):
                        c = half * 8 + j
                        eng = "dve" if j % 2 == 0 else "pool"
                        if eng == "pool":
                            ACT(hT[:, c, b * Lq:(b + 1) * Lq], pstT[:, j, :Lq], AF.Copy, ["pst", "pv", "gfin"], ["hT"],
                                scale=gcol[:, c:c + 1])
                        else:
                            TS_("dve", hT[:, c, b * Lq:(b + 1) * Lq], pstT[:, j, :Lq], gcol[:, c:c + 1], None, ALU.mult, None,
                                ["pst", "pv", "gfin"], ["hT"])

        def final_norm_store(Lq, nblk, dst_rows):
            norm_to_hT(Lq, nblk, gfin)
            enter_group(6)
            for b in range(nblk):
                for half in range(4):
                    p, pk = ps()
                    for j in range(4):
                        c = half * 4 + j
                        MM(p[:Lq, j * 128:(j + 1) * 128], hT[:, c, b * Lq:(b + 1) * Lq], identb[:], True, True,
                           ["hT", "identb"], [pk])
                    CP("act" if half % 2 else "dve", ystage[:Lq, half * 512:(half + 1) * 512], p[:Lq, :], [pk], ["ystage"])
                finals.append(DMA("sp", dst_rows(b), ystage[:Lq, :], ["ystage"], (), "yst"))

        def proj_F(wsrc, c0, ncols, nkc, act_in, NT, evac, in_key):
            done = 0
            while done < ncols:
                ncl = min(512, ncols - done)
                wv, wk = wget(wsrc, 0, nkc, c0 + done, ncl)
                for o in range(ncl // 128):
                    p, pk = ps()
                    for kc in range(nkc):
                        MM(p[:, 0:NT], wv[:, kc, o * 128:(o + 1) * 128], act_in[:, kc, 0:NT], kc == 0, kc == nkc - 1,
                           [wk, in_key], [pk])
                    evac((done // 128) + o, p, pk)
                done += ncl

        def proj_T(wsrc, c0, ncols, nkc, act_in, Lq, nblk, evac, in_key, r0=0):
            done = 0
            while done < ncols:
                ncl = min(512, ncols - done)
                wv, wk = wget(wsrc, r0, nkc, c0 + done, ncl)
                for b in range(nblk):
                    p, pk = ps()
                    for kc in range(nkc):
                        MM(p[:Lq, 0:ncl], act_in[:, kc, b * Lq:(b + 1) * Lq], wv[:, kc, :], kc == 0, kc == nkc - 1,
                           [wk, in_key], [pk])
                    evac(b, done, ncl, p, pk)
                done += ncl

        def do_tile(l, kind, ti):
            sample = kind == "s"
            Lq = TS if sample else 128
            nblk = NSQ if sample else 4
            NT = Lq * nblk
            nseg = nblk if sample else 1
            sl = Lq if sample else 512
            first = (not sample) and ti == 0
            last = (not sample) and ti == NPT - 1
            row0 = SEQ if sample else ti * 512
            win = w_in[l]

            def xrow(b):
                return slice(row0 + b * Lq, row0 + (b + 1) * Lq)

            src = (xs if sample else xp) if l == 0 else xd
            for b in range(nblk):
                if l == 0:
                    rs = slice(b * Lq, (b + 1) * Lq) if sample else xrow(b)
                else:
                    rs = xrow(b)
                DMA("sp", xres[:Lq, b, :], src[rs, :], [("xd", kind, ti, b)], ["xres%d" % b], "xl%d" % b)

            norm_to_hT(Lq, nblk, pv[:, PV_NM:PV_NM + 16])

            if STAGE < 4:
                return
            enter_group(1)

            def ev_q(oc, p, pk):
                CP("act", qT[:, oc, 0:NT], p[:, 0:NT], [pk], ["qT"])
            proj_F(win, C_GQ, 512, 16, hT, NT, ev_q, "hT")

            def ev_k(oc, p, pk):
                CP("dve", kT[:, oc, 0:NT], p[:, 0:NT], [pk], ["kT"])
            proj_F(win, C_GK, 512, 16, hT, NT, ev_k, "hT")

            def ev_kt(b, c0, ncl, p, pk):
                CP("act", ktok[:Lq, b, :], p[:Lq, :], [pk], ["ktok"])
            proj_T(win, C_GK, 512, 16, hT, Lq, nblk, ev_kt, "hT")

            def ev_vt(b, c0, ncl, p, pk):
                CP("dve" if b % 2 else "act", vtok[:Lq, b, c0:c0 + ncl], p[:Lq, 0:ncl], [pk], ["vtok"])
            proj_T(win, C_GV, 1024, 16, hT, Lq, nblk, ev_vt, "hT")

            def ev_gr(oc, p, pk):
                ACT(grT[:, oc, 0:NT], p[:, 0:NT], AF.Silu, [pk], ["grT"])
            proj_F(win, C_GR, 1024, 16, hT, NT, ev_gr, "hT")

            wv, wk = wget(win, 0, 16, C_GLR, 16)
            p, pk = ps()
            for kc in range(16):
                MM(p[0:16, 0:NT], wv[:, kc, 0:16], hT[:, kc, 0:NT], kc == 0, kc == 15, [wk, "hT"], [pk])
            MS("dve", glr[:, 0:NT], 1.0, ["glr"])
            CP("dve", glr[0:16, 0:NT], p[0:16, 0:NT], [pk], ["glr"])

            if STAGE < 5:
                return
            if first:
                MS("dve", Sst[:], 0.0, ["Sst"])
                MS("pool", Sbf[:], 0.0, ["Sbf"])
            for b in range(nblk):
                tk = slice(b * Lq, (b + 1) * Lq)
                if sample:
                    DMA("sp", Sst[:], sgla[l, b].rearrange("h d v -> d h v"), (), ["Sst"], "sst")
                    CP("act", Sbf[:], Sst[:], ["Sst"], ["Sbf"])
                p, pk = ps()
                MM(p[:Lq, :], glr[:, tk], gup[:, :], True, True, ["glr", "gup"], [pk])
                ACT(l1[:Lq, :], p[:Lq, :], AF.Exp, [pk], ["l1"], scale=-1.0)
                ACT(l1[:Lq, :], l1[:Lq, :], AF.Ln, ["l1"], ["l1"], bias=1.0)
                pb, pbk = ps()
                for h in range(4):
                    MM(pb[:, h * Lq:(h + 1) * Lq], l1[:Lq, h * 128:(h + 1) * 128], triS[:Lq, :Lq], True, True,
                       ["l1", "triS"], [pbk])
                pu, puk = ps()
                MM(pu[:Lq, :], triU[:Lq, :Lq], l1[:Lq, :], True, True, ["l1", "triU"], [puk])
                pb3 = pb[:, 0:4 * Lq].rearrange("p (h t) -> p h t", h=4)
                ACT(E1[:, :, :Lq], pb3, AF.Exp, [pbk], ["E1"])
                ACT(E2[:, :, :Lq], pb3, AF.Exp, [pbk], ["E2"], scale=-1.0)
                ACT(E3[:Lq, :], pu[:Lq, :], AF.Exp, [puk], ["E3"])
                STT("dve", qe[:, :, :Lq], qT[:, :, tk], 128 ** -0.5, E1[:, :, :Lq], ALU.mult, ALU.mult, ["qT", "E1"], ["qe"])
                TT("dve", ke[:, :, :Lq], kT[:, :, tk], E2[:, :, :Lq], ALU.mult, ["kT", "E2"], ["ke"])
                TT("pool", kd[:Lq, :], ktok[:Lq, b, :], E3[:Lq, :], ALU.mult, ["ktok", "E3"], ["kd"])
                pa, pak = ps()
                for h in range(4):
                    MM(pa[:Lq, h * Lq:(h + 1) * Lq], ke[:, h, :Lq], qe[:, h, :Lq], True, True, ["ke", "qe"], [pak])
                TT("dve", ATm[:Lq, :, :Lq], pa[:Lq, 0:4 * Lq].rearrange("p (h t) -> p h t", h=4), mask4[:Lq, :, :Lq],
                   ALU.mult, [pak, "mask4"], ["ATm"])
                po = [ps(), ps()]
                for h in range(4):
                    for vc in range(2):
                        i8 = h * 2 + vc
                        pp, ppk = po[i8 // 4]
                        o_ = pp[:, (i8 % 4) * Lq:(i8 % 4 + 1) * Lq]
                        MM(o_, Sbf[:, h, vc * 128:(vc + 1) * 128], qe[:, h, :Lq], True, False, ["Sbf", "qe"], [ppk])
                        MM(o_, vtok[:Lq, b, h * 256 + vc * 128:h * 256 + (vc + 1) * 128], ATm[:Lq, h, :Lq], False, True,
                           ["vtok", "ATm"], [ppk])
                for i in range(2):
                    pp, ppk = po[i]
                    ACT(osq[:, i * 4:(i + 1) * 4, :Lq], pp[:, 0:4 * Lq].rearrange("p (h t) -> p h t", h=4), AF.Square,
                        [ppk], ["osq"])
                pn, pnk = ps()
                for h in range(4):
                    for vc in range(2):
                        MM(pn[:, h * Lq:(h + 1) * Lq], onesb[:, :], osq[:, h * 2 + vc, :Lq], vc == 0, vc == 1,
                           ["onesb", "osq"], [pnk])
                pn3 = pn[:, 0:4 * Lq].rearrange("p (h t) -> p h t", h=4)
                ACT(grs[:, :, :Lq], pn3, AF.Ln, [pnk], ["grs"], scale=1.0 / 256, bias=1e-6)
                ACT(grs[:, :, :Lq], grs[:, :, :Lq], AF.Exp, ["grs"], ["grs"], scale=-0.5)
                for h in range(4):
                    for vc in range(2):
                        i8 = h * 2 + vc
                        pp, ppk = po[i8 // 4]
                        o_ = pp[:, (i8 % 4) * Lq:(i8 % 4 + 1) * Lq]
                        STT("dve", sig[:, 0:Lq], o_, pv[:, PV_GN + vc:PV_GN + vc + 1], grs[:, h, :Lq], ALU.mult, ALU.mult,
                            [ppk, "pv", "grs"], ["sig"])
                        TT("dve", br[:, i8, tk], sig[:, 0:Lq], grT[:, i8, tk], ALU.mult, ["sig", "grT"], ["br"])
                for half in range(2):
                    pp, ppk = ps()
                    for hh in range(2):
                        h = half * 2 + hh
                        MM(pp[:, hh * 256:(hh + 1) * 256], kd[:Lq, h * 128:(h + 1) * 128], vtok[:Lq, b, h * 256:(h + 1) * 256],
                           True, True, ["kd", "vtok"], [ppk])
                    for hh in range(2):
                        h = half * 2 + hh
                        STT("dve", Sst[:, h, :], Sst[:, h, :], E1[:, h, Lq - 1:Lq], pp[:, hh * 256:(hh + 1) * 256],
                            ALU.mult, ALU.add, ["Sst", "E1", ppk], ["Sst"])
                if sample:
                    finals.append(DMA("sp", glas[l, b].rearrange("h d v -> d h v"), Sst[:], ["Sst"], (), "sso"))
                else:
                    CP("act", Sbf[:], Sst[:], ["Sst"], ["Sbf"])
                    if last and b == nblk - 1:
                        finals.append(DMA("sp", glap[l].rearrange("h d v -> d h v"), Sst[:], ["Sst"], (), "sso"))

            def branch_merge(bi):
                wb = w_branch[l, bi]
                for blk4 in range(4):
                    wv, wk = wget(wb, 0, 8, blk4 * 512, 512)
                    pls = []
                    for o in range(4):
                        p, pk = ps()
                        for kc in range(8):
                            MM(p[:, 0:NT], wv[:, kc, o * 128:(o + 1) * 128], br[:, kc, 0:NT], kc == 0, kc == 7, [wk, "br"], [pk])
                        pls.append((p, pk))
                    wg, wgk = wget(win, 0, 16, C_GATE + bi * D + blk4 * 512, 512)
                    for o in range(4):
                        oc = blk4 * 4 + o
                        p2, pk2 = ps()
                        for kc in range(16):
                            MM(p2[:, 0:NT], wg[:, kc, o * 128:(o + 1) * 128], hT[:, kc, 0:NT], kc == 0, kc == 15, [wgk, "hT"], [pk2])
                        ACT(sig[:, 0:NT], p2[:, 0:NT], AF.Sigmoid, [pk2], ["sig"])
                        p, pk = pls[o]
                        if bi == 0:
                            TT("dve", mg[:, oc, 0:NT], p[:, 0:NT], sig[:, 0:NT], ALU.mult, [pk, "sig"], ["mg"])
                        else:
                            TT("dve", tmpb[:, 0:NT], p[:, 0:NT], sig[:, 0:NT], ALU.mult, [pk, "sig"], ["tmpb"])
                            TT("pool", mg[:, oc, 0:NT], mg[:, oc, 0:NT], tmpb[:, 0:NT], ALU.add, ["mg", "tmpb"], ["mg"])

            if STAGE < 6:
                return
            branch_merge(0)
            if STAGE < 7:
                return

            enter_group(2)

            def ev_sq(oc, p, pk):
                CP("act" if oc % 2 else "dve", sqT[:, oc, 0:NT], p[:, 0:NT], [pk], ["sqT"])
            proj_F(win, C_SQ, 1024, 16, hT, NT, ev_sq, "hT")

            def ev_skv(b, c0, ncl, p, pk):
                CP("act", skv[:Lq, b, :], p[:Lq, 0:256], [pk], ["skv"])
            proj_T(win, C_SK, 256, 16, hT, Lq, nblk, ev_skv, "hT")

            for b in range(nblk):
                tk = slice(b * Lq, (b + 1) * Lq)
                has_prev = sample or not (first and b == 0)
                pslot = 0 if sample else b
                cslot = b + 1
                if sample:
                    DMA("sp", stg[:, 0, 0:128], csk[l, b], (), ["stg"], "stg")
                    DMA("sp", stg[:, 1, 0:128], csv[l, b], (), ["stg"], "stg")
                    finals.append(DMA("sp", swaks[l, b, 0:128 - TS, :], csk[l, b, TS:128, :], (), (), "d2d"))
                    finals.append(DMA("sp", swavs[l, b, 0:128 - TS, :], csv[l, b, TS:128, :], (), (), "d2d"))
                    for kv in range(2):
                        for dd in range(2):
                            CP("dve", kdup[:, kv, dd * 64:(dd + 1) * 64], stg[:, 0, kv * 64:(kv + 1) * 64], ["stg"], ["kdup"])
                            CP("pool", Vd[:, 0, kv, dd * 64:(dd + 1) * 64], stg[:, 1, kv * 64:(kv + 1) * 64], ["stg"], ["Vd"])
                    for kv in range(2):
                        TR(pstT[:, kv, :], kdup[:, kv, :], identb[:], ["kdup", "identb"], ["pst"])
                    CP("dve", KTd[:, :, 0:128], pstT[:, 0:2, :], ["pst"], ["KTd"])
                for kv in range(2):
                    for dd in range(2):
                        CP("dve", kdup[:Lq, kv, dd * 64:(dd + 1) * 64], skv[:Lq, b, kv * 64:(kv + 1) * 64], ["skv"], ["kdup"])
                        CP("pool", Vd[:Lq, cslot, kv, dd * 64:(dd + 1) * 64], skv[:Lq, b, 128 + kv * 64:128 + (kv + 1) * 64],
                           ["skv"], ["Vd"])
                for kv in range(2):
                    TR(pstT[:, kv, :Lq], kdup[:Lq, kv, :], identb[:Lq, :Lq], ["kdup", "identb"], ["pst"])
                CP("dve", KTd[:, :, cslot * 128:cslot * 128 + Lq], pstT[:, 0:2, :Lq], ["pst"], ["KTd"])
                if sample:
                    finals.append(DMA("sp", swaks[l, b, 128 - TS:128, :], skv[:Lq, b, 0:128], ["skv"], (), "skvo"))
                    finals.append(DMA("sp", swavs[l, b, 128 - TS:128, :], skv[:Lq, b, 128:256], ["skv"], (), "skvo"))
                elif last and b == nblk - 1:
                    finals.append(DMA("sp", swakp[l], skv[:Lq, b, 0:128], ["skv"], (), "skvo"))
                    finals.append(DMA("sp", swavp[l], skv[:Lq, b, 128:256], ["skv"], (), "skvo"))
                for kv in range(2):
                    for par in range(2):
                        hs = slice(par * 64, (par + 1) * 64)
                        hsel = slice(kv * 8 + par, kv * 8 + 8, 2)
                        qrhs = sqT[hs, kv * 4:kv * 4 + 4, tk]
                        pc, pck = ps()
                        MM(pc[:Lq, 0:4 * Lq].rearrange("p (h t) -> p h t", h=4), KTd[hs, kv, cslot * 128:cslot * 128 + Lq], qrhs,
                           True, True, ["KTd", "sqT"], [pck])
                        STT("dve", sbias[:Lq, 0:4 * Lq].rearrange("p (h t) -> p h t", h=4),
                            pc[:Lq, 0:4 * Lq].rearrange("p (h t) -> p h t", h=4), 0.125, biasC[:Lq, hsel, :Lq],
                            ALU.mult, ALU.add, [pck, "bias"], ["sbias"])
                        ACT(PTc[:Lq, :, :Lq], sbias[:Lq, 0:4 * Lq].rearrange("p (h t) -> p h t", h=4), AF.Exp, ["sbias"], ["PTc"])
                        if has_prev:
                            pp_, ppk = ps()
                            MM(pp_[:, 0:4 * Lq].rearrange("p (h t) -> p h t", h=4), KTd[hs, kv, pslot * 128:(pslot + 1) * 128], qrhs,
                               True, True, ["KTd", "sqT"], [ppk])
                            STT("dve", sbias[:, 0:4 * Lq].rearrange("p (h t) -> p h t", h=4),
                                pp_[:, 0:4 * Lq].rearrange("p (h t) -> p h t", h=4), 0.125, biasP[:, hsel, :Lq],
                                ALU.mult, ALU.add, [ppk, "bias"], ["sbias"])
                            ACT(PTp[:, :, :Lq], sbias[:, 0:4 * Lq].rearrange("p (h t) -> p h t", h=4), AF.Exp, ["sbias"], ["PTp"])
                        po_, pok = ps()
                        pd_, pdk = ps()
                        for j in range(4):
                            o_ = po_[:, j * Lq:(j + 1) * Lq]
                            if has_prev:
                                MM(o_, Vd[:, pslot, kv, :], PTp[:, j, :Lq], True, False, ["Vd", "PTp"], [pok])
                            MM(o_, Vd[:Lq, cslot, kv, :], PTc[:Lq, j, :Lq], not has_prev, True, ["Vd", "PTc"], [pok])
                        for j in range(4):
                            d_ = pd_[:, j * Lq:(j + 1) * Lq]
                            if has_prev:
                                MM(d_, onesb[:, :], PTp[:, j, :Lq], True, False, ["onesb", "PTp"], [pdk])
                            MM(d_, onesb[:Lq, :], PTc[:Lq, j, :Lq], not has_prev, True, ["onesb", "PTc"], [pdk])
                        for j in range(4):
                            hh = kv * 8 + par + 2 * j
                            TS_("dve", rden[:, j, :Lq], pd_[:, j * Lq:(j + 1) * Lq], esink[:, hh:hh + 1], None, ALU.add, None,
                                [pdk, "esink"], ["rden"])
                        S.add("dve", lambda e: e.reciprocal(out=rden[:, :, :Lq], in_=rden[:, :, :Lq]), ["rden"], ["rden"])
                        TT("dve", br[hs, kv * 4:kv * 4 + 4, tk], po_[hs, 0:4 * Lq].rearrange("p (h t) -> p h t", h=4),
                           rden[hs, :, :Lq], ALU.mult, [pok, "rden"], ["br"])
            if not sample:
                CP("dve", KTd[:, :, 0:128], KTd[:, :, 4 * 128:5 * 128], ["KTd"], ["KTd"])
                CP("pool", Vd[:, 0, :, :], Vd[:, 4, :, :], ["Vd"], ["Vd"])
            if STAGE < 8:
                return
            branch_merge(1)

            enter_group(3)

            def tail_load(src2, nch, dstbuf, s_, dkey):
                n2 = 2 * nch
                DMA("sp", tl[0:n2, :], src2.rearrange("r (c f) -> (r c) f", f=128), (), ["tl"], "tl")
                p, pk = ps()
                MM(p[:, 0:n2], tl[0:n2, :], identf[0:n2, 0:n2], True, True, ["tl", "identf"], [pk])
                CP("dve", dstbuf[:, 0:nch, s_, :], p[:, 0:n2].rearrange("p (r c) -> p c r", r=2), [pk], [dkey])

            def tail_store(srcbuf, nch, s_, skey, dst2):
                n2 = 2 * nch
                CP("dve", tls[:, 0:n2].rearrange("p (r c) -> p c r", r=2), srcbuf[:, 0:nch, s_, :], [skey], ["tls"])
                p, pk = ps()
                MM(p[0:n2, 0:128], tls[:, 0:n2], identf[:, :], True, True, ["tls", "identf"], [pk])
                CP("dve", tlo[0:n2, :], p[0:n2, 0:128], [pk], ["tlo"])
                finals.append(DMA("sp", dst2.rearrange("r (c f) -> (r c) f", f=128), tlo[0:n2, :], ["tlo"], (), "tlo"))

            def ev_cb(oc, p, pk):
                CP("act", cbT[:, oc, 0:NT], p[:, 0:NT], [pk], ["cbT"])
            proj_F(win, C_CB, 1024, 16, hT, NT, ev_cb, "hT")

            def ev_cc(oc, p, pk):
                CP("act", ccT[:, oc, 0:NT], p[:, 0:NT], [pk], ["ccT"])
            proj_F(win, C_CC, 1024, 16, hT, NT, ev_cc, "hT")

            up4 = upad[:, :, 0:nseg * (sl + 2)].rearrange("p c (s t) -> p c s t", s=nseg)
            if first:
                MS("dve", ctail[:], 0.0, ["ctail"])
            if sample:
                for s_ in range(nseg):
                    tail_load(sconv[l, s_], 8, ctail, s_, "ctail")
            CP("dve", up4[:, :, :, 0:2], ctail[:, :, 0:nseg, :], ["ctail"], ["upad"])

            def ev_ch(oc, p, pk):
                TT("dve", up4[:, oc, :, 2:sl + 2], p[:, 0:NT].rearrange("p (s t) -> p s t", s=nseg),
                   ccT[:, oc, 0:NT].rearrange("p (s t) -> p s t", s=nseg), ALU.mult, [pk, "ccT"], ["upad"])
                c3 = ctmp[:, 0:NT].rearrange("p (s t) -> p s t", s=nseg)
                TS_("pool", c3, up4[:, oc, :, 0:sl], pv[:, PV_CW + oc:PV_CW + oc + 1], None, ALU.mult, None, ["upad", "pv"], ["ctmp"])
                STT("dve", c3, up4[:, oc, :, 1:sl + 1], pv[:, PV_CW + 8 + oc:PV_CW + 8 + oc + 1], c3, ALU.mult, ALU.add,
                    ["upad", "pv", "ctmp"], ["ctmp"])
                STT("dve", c3, up4[:, oc, :, 2:sl + 2], pv[:, PV_CW + 16 + oc:PV_CW + 16 + oc + 1], c3, ALU.mult, ALU.add,
                    ["upad", "pv", "ctmp"], ["ctmp"])
                TT("pool", br[:, oc, 0:NT], ctmp[:, 0:NT], cbT[:, oc, 0:NT], ALU.mult, ["ctmp", "cbT"], ["br"])
            proj_F(win, C_CH, 1024, 16, hT, NT, ev_ch, "hT")
            CP("dve", ctail[:, :, 0:nseg, :], up4[:, :, :, sl:sl + 2], ["upad"], ["ctail"])
            if sample:
                for s_ in range(nseg):
                    tail_store(ctail, 8, s_, "ctail", convs[l, s_])
            elif last:
                tail_store(ctail, 8, 0, "ctail", convp[l])
            if STAGE < 9:
                return
            branch_merge(2)

            def ev_res(b, c0, ncl, p, pk):
                TT("dve", xres[:Lq, b, c0:c0 + ncl], xres[:Lq, b, c0:c0 + ncl], p[:Lq, 0:ncl], ALU.add, [pk, "xres%d" % b], ["xres%d" % b])
            proj_T(w_out[l], 0, D, 16, mg, Lq, nblk, ev_res, "mg")

            if STAGE < 10:
                return
            norm_to_hT(Lq, nblk, pv[:, PV_NX:PV_NX + 16])

            enter_group(4)

            def ev_qx(oc, p, pk):
                CP("act", qxT[:, oc, 0:NT], p[:, 0:NT], [pk], ["qxT"])
            proj_F(wx_q[l], 0, 512, 16, hT, NT, ev_qx, "hT")
            for b in range(nblk):
                tk = slice(b * Lq, (b + 1) * Lq)
                if sample:
                    DMA("sp", stgx[:], cmk[l, b].rearrange("(c p) n -> p c n", p=128), (), ["stgx"], "stgx")
                    for mc in range(2):
                        CP("dve", vtokx[:, mc, :], stgx[:, mc, :], ["stgx"], ["vtokx"])
                    for mc in range(2):
                        for h in range(4):
                            TR(pstT[:, mc * 4 + h, :], vtokx[:, mc, h * 128:(h + 1) * 128], identb[:], ["vtokx", "identb"], ["pst"])
                    for h in range(4):
                        for mc in range(2):
                            CP("dve", KxT[:, h, mc * 128:(mc + 1) * 128], pstT[:, mc * 4 + h, :], ["pst"], ["KxT"])
                    DMA("sp", stgx[:], cmv[l, b].rearrange("(c p) n -> p c n", p=128), (), ["stgx"], "stgx")
                    CP("pool", Vx[:], stgx[:], ["stgx"], ["Vx"])
                for h in range(4):
                    pp_, ppk = ps()
                    for mc in range(2):
                        MM(pp_[:, mc * Lq:(mc + 1) * Lq], KxT[:, h, mc * 128:(mc + 1) * 128], qxT[:, h, tk], True, True, ["KxT", "qxT"], [ppk])
                    ACT(PTx[:, :, :Lq], pp_[:, 0:2 * Lq].rearrange("p (c t) -> p c t", c=2), AF.Exp, [ppk], ["PTx"], scale=128 ** -0.5)
                    po_, pok = ps()
                    for mc in range(2):
                        MM(po_[:, 0:Lq], Vx[:, mc, h * 128:(h + 1) * 128], PTx[:, mc, :Lq], mc == 0, mc == 1, ["Vx", "PTx"], [pok])
                    for mc in range(2):
                        MM(po_[:, 128:128 + Lq], onesb[:, :], PTx[:, mc, :Lq], mc == 0, mc == 1, ["onesb", "PTx"], [pok])
                    S.add("dve", lambda e, po_=po_: e.reciprocal(out=rdenx[:, :Lq], in_=po_[:, 128:128 + Lq]), [pok], ["rdenx"])
                    TT("dve", oxT[:, h, tk], po_[:, 0:Lq], rdenx[:, :Lq], ALU.mult, [pok, "rdenx"], ["oxT"])
            proj_T(wx_o[l], 0, D, 4, oxT, Lq, nblk, ev_res, "oxT")

            if STAGE < 11:
                return
            norm_to_hT(Lq, nblk, pv[:, PV_NF:PV_NF + 16])
            enter_group(5)
            gt4 = gtail[:, :, 0:nseg, :]
            if first:
                MS("dve", gtail[:], 0.0, ["gtail"])
            if sample:
                for s_ in range(nseg):
                    tail_load(sffn[l, s_], 43, gtail, s_, "gtail")
            gp3 = gpad[:, 0:nseg * (sl + 2)].rearrange("p (s t) -> p s t", s=nseg)
            wup = ffn_up[l]
            wdn = ffn_down[l]
            for piece in range(3):
                c_lo = piece * 16
                c_hi = min(43, c_lo + 16)
                ncp = c_hi - c_lo
                cc = c_lo
                while cc < c_hi:
                    n = min(4, c_hi - cc)
                    wu, wuk = wget(wup, 0, 16, cc * 128, n * 128)
                    pus = []
                    for o in range(n):
                        p, pk = ps()
                        for kc in range(16):
                            MM(p[:, 0:NT], wu[:, kc, o * 128:(o + 1) * 128], hT[:, kc, 0:NT], kc == 0, kc == 15, [wuk, "hT"], [pk])
                        pus.append((p, pk))
                    wg_, wgk = wget(wup, 0, 16, DFF + cc * 128, n * 128)
                    for o in range(n):
                        c = cc + o
                        p2, pk2 = ps()
                        for kc in range(16):
                            MM(p2[:, 0:NT], wg_[:, kc, o * 128:(o + 1) * 128], hT[:, kc, 0:NT], kc == 0, kc == 15, [wgk, "hT"], [pk2])
                        CP("pool", gp3[:, :, 0:2], gt4[:, c, :, :], ["gtail"], ["gpad"])
                        ACT(gp3[:, :, 2:sl + 2], p2[:, 0:NT].rearrange("p (s t) -> p s t", s=nseg), AF.Copy, [pk2], ["gpad"])
                        CP("pool", gt4[:, c, :, :], gp3[:, :, sl:sl + 2], ["gpad"], ["gtail"])
                        c3 = ctmp5[:, 0:NT].rearrange("p (s t) -> p s t", s=nseg)
                        TS_("pool", c3, gp3[:, :, 0:sl], pv[:, PV_FW + c:PV_FW + c + 1], None, ALU.mult, None, ["gpad", "pv"], ["ctmp5"])
                        STT("dve", c3, gp3[:, :, 1:sl + 1], pv[:, PV_FW + 43 + c:PV_FW + 43 + c + 1], c3, ALU.mult, ALU.add,
                            ["gpad", "pv", "ctmp5"], ["ctmp5"])
                        STT("dve", c3, gp3[:, :, 2:sl + 2], pv[:, PV_FW + 86 + c:PV_FW + 86 + c + 1], c3, ALU.mult, ALU.add,
                            ["gpad", "pv", "ctmp5"], ["ctmp5"])
                        ACT(sig[:, 0:NT], ctmp5[:, 0:NT], AF.Silu, ["ctmp5", "pv"], ["sig"], bias=pv[:, PV_FB + c:PV_FB + c + 1])
                        p, pk = pus[o]
                        TT("dve", mg[:, c - c_lo, 0:NT], p[:, 0:NT], sig[:, 0:NT], ALU.mult, [pk, "sig"], ["mg"])
                    cc += n
                proj_T(wdn, 0, D, ncp, mg, Lq, nblk, ev_res, "mg", r0=c_lo)
            if sample:
                for s_ in range(nseg):
                    tail_store(gtail, 43, s_, "gtail", ffns[l, s_])
            elif last:
                tail_store(gtail, 43, 0, "gtail", ffnp[l])

            if STAGE < 12:
                return
            if l == DEPTH - 1:
                dst = ys if sample else yp

                def rows(b):
                    if sample:
                        return dst[b * Lq:(b + 1) * Lq, :]
                    return dst[xrow(b), :]
                final_norm_store(Lq, nblk, rows)
            else:
                for b in range(nblk):
                    DMA("sp", xd[xrow(b), :], xres[:Lq, b, :], ["xres%d" % b], [("xd", kind, ti, b)], "xst%d" % b)

        def mem_kv(l):
            for mb in range(2):
                DMA("sp", xres[:, mb, :], memp[mb * 128:(mb + 1) * 128, :], (), ["xres%d" % mb], "xl%d" % mb)
                CP("dve", h0[:, :], xres[:, mb, :], ["xres%d" % mb], ["h0"])
                for half in range(2):
                    for j in range(8):
                        c = half * 8 + j
                        TR(pstT[:, j, :], h0[:, c * 128:(c + 1) * 128], identb[:], ["h0", "identb"], ["pst"])
                    CP("dve", hT[:, half * 8:(half + 1) * 8, mb * 128:(mb + 1) * 128], pstT[:, :, :], ["pst"], ["hT"])

            enter_group(6)

            def ev_kx(oc, p, pk):
                CP("act", KxT[:, oc, :], p[:, 0:256], [pk], ["KxT"])
            proj_F(wx_k[l], 0, 512, 16, hT, 256, ev_kx, "hT")

            def ev_kt(b, c0, ncl, p, pk):
                CP("act", ystage[:, 0:512], p[:, :], [pk], ["ystage"])
                finals.append(DMA("sp", memkp[l, b * 128:(b + 1) * 128, :], ystage[:, 0:512], ["ystage"], (), "yst"))
            proj_T(wx_k[l], 0, 512, 16, hT, 128, 2, ev_kt, "hT")

            def ev_vt(b, c0, ncl, p, pk):
                CP("act", ystage[:, 0:512], p[:, :], [pk], ["ystage"])
                CP("dve", Vx[:, b, :], p[:, :], [pk], ["Vx"])
                finals.append(DMA("sp", memvp[l, b * 128:(b + 1) * 128, :], ystage[:, 0:512], ["ystage"], (), "yst"))
            proj_T(wx_v[l], 0, 512, 16, hT, 128, 2, ev_vt, "hT")

        def emit_all():
            ps_i[0] = 0
            setup()
            if STAGE < 1:
                return
            for l in range(DEPTH):
                layer_setup(l)
                if STAGE < 2:
                    return
                mem_kv(l)
                if STAGE < 3:
                    return
                for ti in range(NPT):
                    do_tile(l, "p", ti)
                if STAGE < 20:
                    return
                do_tile(l, "s", 0)

        S.planning = True
        emit_all()
        S.planning = False
        del finals[:]
        emit_all()
        assert WS.pos == len(WS.plan), (WS.pos, len(WS.plan))
        S.finish(finals)
    return nc


_CACHE = {}


def run(inputs, SEQ, DEPTH, NB, DEC_BATCH, TS=8):
    NSQ = DEC_BATCH // 8
    key = (SEQ, DEPTH, NSQ, TS, STAGE)
    if key not in _CACHE:
        _CACHE[key] = build(SEQ, DEPTH, NSQ, TS)
    nc = _CACHE[key]
    f = lambda a: np.ascontiguousarray(np.asarray(a, dtype=np.float32))
    consts = host_consts()
    shared = {}
    for k in ("norm_mix", "w_in", "gla_gate_up", "gla_gate_b", "gla_norm", "swa_sinks", "rel_bias", "conv_w",
              "w_branch", "w_out", "norm_x", "wx_q", "wx_k", "wx_v", "wx_o", "norm_ffn", "ffn_up", "ffn_conv_w",
              "ffn_conv_b", "ffn_down", "norm_final"):
        shared[k] = f(inputs[k])
    shared.update(consts)
    xpr = f(inputs["x_prompt"])
    xsm = f(inputs["x_sample"])
    in_maps = []
    for c in range(8):
        pb = c % NB
        ss = slice(c * NSQ, (c + 1) * NSQ)
        m = dict(shared)
        m["xp"] = xpr[pb]
        m["xs"] = f(xsm[ss].reshape(NSQ * TS, D))
        m["sgla"] = f(inputs["state_gla"][:, ss])
        m["csk"] = f(np.asarray(inputs["cache_swa_k"])[:, ss].reshape(DEPTH, NSQ, 128, 128))
        m["csv"] = f(np.asarray(inputs["cache_swa_v"])[:, ss].reshape(DEPTH, NSQ, 128, 128))
        m["sconv"] = f(inputs["state_conv"][:, ss])
        m["sffn"] = f(inputs["state_ffn"][:, ss])
        m["cmk"] = f(np.asarray(inputs["cache_mem_k"])[:, ss].reshape(DEPTH, NSQ, 256, 512))
        m["cmv"] = f(np.asarray(inputs["cache_mem_v"])[:, ss].reshape(DEPTH, NSQ, 256, 512))
        m["memp"] = f(inputs["mem_prompt"][pb])
        in_maps.append(m)
    res = run_bass_kernel_spmd(nc, in_maps, core_ids=list(range(8)))
    R = res.results
    cat = lambda k, ax: np.concatenate([R[c][k] for c in range(8)], axis=ax)
    stk = lambda k: np.stack([R[c][k] for c in range(NB)], axis=1)
    y_prompt = np.stack([R[c]["yp"] for c in range(NB)], axis=0)
    y_sample = cat("ys", 0).reshape(DEC_BATCH, TS, D)
    outs = (
        y_prompt, y_sample,
        stk("glap"), cat("glas", 1),
        stk("swakp").reshape(DEPTH, NB, 128, 2, 64), stk("swavp").reshape(DEPTH, NB, 128, 2, 64),
        cat("swaks", 1).reshape(DEPTH, DEC_BATCH, 128, 2, 64), cat("swavs", 1).reshape(DEPTH, DEC_BATCH, 128, 2, 64),
        stk("convp"), cat("convs", 1), stk("ffnp"), cat("ffns", 1),
        stk("memkp").reshape(DEPTH, NB, 256, 4, 128), stk("memvp").reshape(DEPTH, NB, 256, 4, 128),
    )
    return tuple(np.ascontiguousarray(o, dtype=np.float32) for o in outs)


def kernel(**inputs):
    return run(inputs, 4096, 4, 2, 32)
```
